# Optimizing a Trainium2 kernel written in Bass

```python
import math
import jax, jax.numpy as jnp
from jax import lax
import numpy as np

D_MODEL = 1024
BATCH = 8
SEQ = 4096
DEPTH = 4

CHUNK = 64
N_MEM = 256
N_MIXERS = 2
SSM_GROUP = 16
SSM_GROUPS = D_MODEL // SSM_GROUP
SSM_STATE = 64
DT_MIN = 1e-3
DT_MAX = 1e-1
CONV_WIDTH = 3
XATTN_HEADS = 4
XATTN_HEAD_DIM = D_MODEL // XATTN_HEADS
MLP_HIDDEN = 4 * D_MODEL
N_SSM_LAYERS = (DEPTH + 1) // 2
N_CONV_LAYERS = DEPTH // 2
NORM_EPS = 1e-6

kernel_name = "hybrid_s5_shortconv_memxattn_trunk"


def _rms_norm(x, g):
    x32 = x.astype(jnp.float32)
    y = x32 * lax.rsqrt(jnp.mean(x32 * x32, axis=-1, keepdims=True) + NORM_EPS)
    return (y * g.astype(jnp.float32)).astype(x.dtype)


def _diag_combine(left, right):
    a1, b1 = left
    a2, b2 = right
    return a1 * a2, a2 * b1 + b2


def _s5_mixer(h, a_re, a_im, log_dt, b_re, b_im, c_re, c_im, d_skip, w_glu):
    bsz, seq, dm = h.shape
    n_chunks = seq // CHUNK
    f32 = jnp.float32
    u = h.astype(f32)
    lam = lax.complex(a_re.astype(f32), a_im.astype(f32))
    dt = jnp.exp(log_dt.astype(f32))[:, None]
    a_bar = jnp.exp(lam * dt)
    b = lax.complex(b_re.astype(f32), b_im.astype(f32))
    b_bar = ((a_bar - 1.0) / lam)[:, :, None] * b
    c = lax.complex(c_re.astype(f32), c_im.astype(f32))
    u_chunks = u.reshape(bsz, n_chunks, CHUNK, SSM_GROUPS, SSM_GROUP).transpose(1, 0, 2, 3, 4)
    a_seq = jnp.broadcast_to(a_bar, (CHUNK, bsz, SSM_GROUPS, SSM_STATE))

    def chunk_step(state, u_c):
        bu = jnp.einsum('bcgh,gph->cbgp', u_c.astype(jnp.complex64), b_bar)
        bu = bu.at[0].add(a_bar * state)
        _, states = lax.associative_scan(_diag_combine, (a_seq, bu), axis=0)
        y_c = jnp.einsum('cbgp,ghp->bcgh', states, c).real
        return states[-1], y_c

    state0 = jnp.zeros((bsz, SSM_GROUPS, SSM_STATE), jnp.complex64)
    _, ys = lax.scan(chunk_step, state0, u_chunks)
    y = ys.transpose(1, 0, 2, 3, 4).reshape(bsz, seq, dm) + d_skip.astype(f32) * u
    y = jax.nn.gelu(y).astype(h.dtype)
    val, gate = jnp.split(y @ w_glu, 2, axis=-1)
    return val * jax.nn.sigmoid(gate)


def _short_conv_mixer(h, w_in, conv_w, w_out):
    dm = h.shape[-1]
    gate_b, gate_c, v = jnp.split(h @ w_in, 3, axis=-1)
    z = gate_c * v
    z = lax.conv_general_dilated(
        z, conv_w[:, None, :].astype(z.dtype), window_strides=(1,),
        padding=[(CONV_WIDTH - 1, 0)], dimension_numbers=('NWC', 'WIO', 'NWC'),
        feature_group_count=dm)
    return (gate_b * z) @ w_out


def _memory_cross_attention(h, mem_n, w_q, w_kv, w_o):
    bsz, seq, dm = h.shape
    n_mem = mem_n.shape[1]
    q = (h @ w_q).reshape(bsz, seq, XATTN_HEADS, XATTN_HEAD_DIM)
    k, v = jnp.split(mem_n @ w_kv, 2, axis=-1)
    k = k.reshape(bsz, n_mem, XATTN_HEADS, XATTN_HEAD_DIM)
    v = v.reshape(bsz, n_mem, XATTN_HEADS, XATTN_HEAD_DIM)
    s = jnp.einsum('bqhd,bkhd->bhqk', q.astype(jnp.float32), k.astype(jnp.float32)) * (XATTN_HEAD_DIM ** -0.5)
    p = jax.nn.softmax(s, axis=-1)
    o = jnp.einsum('bhqk,bkhd->bqhd', p, v.astype(jnp.float32)).reshape(bsz, seq, dm)
    return o.astype(h.dtype) @ w_o


def _sqrelu_mlp(h, w1, w2):
    a = jax.nn.relu(h @ w1)
    return (a * a) @ w2


def setup_inputs(seed: int = 0) -> dict:
    key = jax.random.key(seed)
    ks = jax.random.split(key, 24)
    f32 = jnp.float32
    D, G, P, H = D_MODEL, SSM_GROUPS, SSM_STATE, SSM_GROUP
    Ls, Lc = N_SSM_LAYERS, N_CONV_LAYERS

    def nrm(k, shape, scale):
        return jax.random.normal(k, shape, f32) * scale

    def gain(k, shape):
        return 1.0 + 0.02 * jax.random.normal(k, shape, f32)

    a_im_base = jnp.pi * jnp.arange(P, dtype=f32)
    return {
        "x": jax.random.normal(ks[0], (BATCH, SEQ, D), f32),
        "mem": jax.random.normal(ks[1], (BATCH, N_MEM, D), f32),
        "mem_norm_g": gain(ks[2], (D,)),
        "mix_norm_g": gain(ks[3], (DEPTH, D)),
        "xattn_norm_g": gain(ks[4], (DEPTH, D)),
        "mlp_norm_g": gain(ks[5], (DEPTH, D)),
        "s5_a_re": -0.5 + 0.01 * jax.random.normal(ks[6], (Ls, G, P), f32),
        "s5_a_im": a_im_base + 0.01 * jax.random.normal(ks[7], (Ls, G, P), f32),
        "s5_log_dt": jax.random.uniform(ks[8], (Ls, G), f32, math.log(DT_MIN), math.log(DT_MAX)),
        "s5_b_re": nrm(ks[9], (Ls, G, P, H), (2.0 * H) ** -0.5),
        "s5_b_im": nrm(ks[10], (Ls, G, P, H), (2.0 * H) ** -0.5),
        "s5_c_re": nrm(ks[11], (Ls, G, H, P), P ** -0.5),
        "s5_c_im": nrm(ks[12], (Ls, G, H, P), P ** -0.5),
        "s5_d": nrm(ks[13], (Ls, D), 1.0),
        "s5_w_glu": nrm(ks[14], (Ls, D, 2 * D), D ** -0.5),
        "conv_w_in": nrm(ks[15], (Lc, D, 3 * D), D ** -0.5),
        "conv_w": nrm(ks[16], (Lc, CONV_WIDTH, D), CONV_WIDTH ** -0.5),
        "conv_w_out": nrm(ks[17], (Lc, D, D), D ** -0.5),
        "xa_w_q": nrm(ks[18], (DEPTH, D, D), D ** -0.5),
        "xa_w_kv": nrm(ks[19], (DEPTH, D, 2 * D), D ** -0.5),
        "xa_w_o": nrm(ks[20], (DEPTH, D, D), D ** -0.5),
        "mlp_w1": nrm(ks[21], (DEPTH, D, MLP_HIDDEN), D ** -0.5),
        "mlp_w2": nrm(ks[22], (DEPTH, MLP_HIDDEN, D), MLP_HIDDEN ** -0.5),
        "final_norm_g": gain(ks[23], (D,)),
    }


def reference(x, mem, mem_norm_g, mix_norm_g, xattn_norm_g, mlp_norm_g,
              s5_a_re, s5_a_im, s5_log_dt, s5_b_re, s5_b_im, s5_c_re, s5_c_im,
              s5_d, s5_w_glu, conv_w_in, conv_w, conv_w_out,
              xa_w_q, xa_w_kv, xa_w_o, mlp_w1, mlp_w2, final_norm_g):
    mem_n = _rms_norm(mem, mem_norm_g)
    for i in range(DEPTH):
        h = _rms_norm(x, mix_norm_g[i])
        j = i // N_MIXERS
        if i % N_MIXERS == 0:
            x = x + _s5_mixer(h, s5_a_re[j], s5_a_im[j], s5_log_dt[j], s5_b_re[j], s5_b_im[j],
                              s5_c_re[j], s5_c_im[j], s5_d[j], s5_w_glu[j])
        else:
            x = x + _short_conv_mixer(h, conv_w_in[j], conv_w[j], conv_w_out[j])
        x = x + _memory_cross_attention(_rms_norm(x, xattn_norm_g[i]), mem_n,
                                        xa_w_q[i], xa_w_kv[i], xa_w_o[i])
        x = x + _sqrelu_mlp(_rms_norm(x, mlp_norm_g[i]), mlp_w1[i], mlp_w2[i])
    return _rms_norm(x, final_norm_g)
```

```python
import contextlib
import math
import numpy as np
import concourse.bass as bass
import concourse.mybir as mybir
from concourse.bass_utils import run_bass_kernel_spmd

F32 = mybir.dt.float32
BF16 = mybir.dt.bfloat16
I32 = mybir.dt.int32
AF = mybir.ActivationFunctionType
ALU = mybir.AluOpType

D = 1024
SEQ = 4096
NMEM = 256
DEPTH = 4
HID = 4096
EPS = 1e-6
NHB = 8
ENGS = ["pe", "act", "dve", "pool", "sp"]


class _Ins:
    __slots__ = ("fn", "deps", "inc", "dma_sem", "dma_cnt", "eng", "idx")

    def __init__(self, fn, eng, idx):
        self.fn = fn
        self.deps = []
        self.inc = False
        self.dma_sem = None
        self.dma_cnt = 0
        self.eng = eng
        self.idx = idx


class Prog:
    def __init__(self, nc):
        self.nc = nc
        self.ins = {e: [] for e in ENGS}
        self.last_w = {}
        self.readers = {}
        self.dma_counts = {}
        self.final_tokens = []
        self.pending = {e: [] for e in ENGS}
        self.pending_c = {e: [] for e in ENGS}

    def _collect(self, eng, reads, writes, is_dma):
        deps = []
        for r in reads:
            t = self.last_w.get(r)
            if t is not None:
                deps.append(("raw", t))
        for w in writes:
            t = self.last_w.get(w)
            if t is not None:
                deps.append(("waw", t))
            for t in self.readers.get(w, ()):
                deps.append(("war", t))
        out = []
        for kind, t in deps:
            if t[0] == "c" and t[1] == eng:
                if eng == "pe":
                    continue
            out.append(t)
        if self.pending[eng]:
            out.extend(self.pending[eng])
            self.pending[eng] = []
        if self.pending_c[eng] and not is_dma:
            out.extend(self.pending_c[eng])
            self.pending_c[eng] = []
        return out

    def _register(self, tok, reads, writes):
        for w in writes:
            self.last_w[w] = tok
            self.readers[w] = []
        for r in reads:
            self.readers.setdefault(r, []).append(tok)

    def op(self, eng, fn, reads=(), writes=()):
        lst = self.ins[eng]
        ins = _Ins(fn, eng, len(lst))
        ins.deps = self._collect(eng, reads, writes, False)
        lst.append(ins)
        tok = ("c", eng, ins.idx)
        self._register(tok, reads, writes)
        return tok

    def dma(self, eng, out, in_, reads=(), writes=(), semkey=None, final=False, **kw):
        if semkey is None:
            semkey = writes[0]
        lst = self.ins[eng]

        def fn(e, out=out, in_=in_, kw=kw):
            return e.dma_start(out=out, in_=in_, **kw)

        ins = _Ins(fn, eng, len(lst))
        ins.deps = self._collect(eng, reads, writes, True)
        cnt = self.dma_counts.get(semkey, 0) + 16
        self.dma_counts[semkey] = cnt
        ins.dma_sem = semkey
        ins.dma_cnt = cnt
        lst.append(ins)
        tok = ("d", semkey, cnt)
        self._register(tok, reads, writes)
        if final:
            self.final_tokens.append(tok)
        return tok

    def barrier(self, bar_aps, light=False):
        toks = []
        for e in ("act", "dve", "pool"):
            ap = bar_aps[e]
            if e == "act":
                src = bar_aps["src"]
                t = self.op(e, lambda en, ap=ap, src=src: en.activation(out=ap, in_=src, func=AF.Copy), ["ident_f"])
            else:
                t = self.op(e, lambda en, ap=ap: en.memset(ap, 0.0))
            toks.append(t)
        if self.ins["pe"]:
            toks.append(("c", "pe", len(self.ins["pe"]) - 1))
        if light:
            for e in ("pe", "act", "dve"):
                self.pending_c[e] = self.pending_c[e] + list(toks)
            self.pending["pool"] = self.pending["pool"] + list(toks)
            return
        for k, c in self.dma_counts.items():
            toks.append(("d", k, c))
        for e in ENGS:
            self.pending[e] = list(toks)

    def emit(self):
        nc = self.nc
        for e in ENGS:
            for ins in self.ins[e]:
                for t in ins.deps:
                    if t[0] == "c":
                        self.ins[t[1]][t[2]].inc = True
        rank = {}
        for e in ENGS:
            c = 0
            for ins in self.ins[e]:
                if ins.inc and ins.dma_sem is None:
                    c += 1
                    rank[(e, ins.idx)] = c
        with contextlib.ExitStack() as st:
            esem = {e: st.enter_context(nc.semaphore("prog_" + e)) for e in ENGS}
            dsem = {}
            for i, k in enumerate(self.dma_counts):
                dsem[k] = st.enter_context(nc.semaphore("dma_%d" % i))
            block = st.enter_context(nc.Block())

            def run(engname, eng):
                known_c = {}
                known_d = {}
                for ins in self.ins[engname]:
                    need_c = {}
                    need_d = {}
                    for t in ins.deps:
                        if t[0] == "c":
                            v = rank[(t[1], t[2])]
                            if v > known_c.get(t[1], 0):
                                need_c[t[1]] = max(need_c.get(t[1], 0), v)
                        else:
                            if t[2] > known_d.get(t[1], 0):
                                need_d[t[1]] = max(need_d.get(t[1], 0), t[2])
                    for e2, v in need_c.items():
                        eng.wait_ge(esem[e2], v)
                        known_c[e2] = v
                    for k, v in need_d.items():
                        eng.wait_ge(dsem[k], v)
                        known_d[k] = v
                    bi = ins.fn(eng)
                    if ins.dma_sem is not None:
                        bi.then_inc(dsem[ins.dma_sem], 16)
                    elif ins.inc:
                        bi.then_inc(esem[engname], 1)
                if engname == "sp":
                    for t in self.final_tokens:
                        if t[2] > known_d.get(t[1], 0):
                            eng.wait_ge(dsem[t[1]], t[2])
                            known_d[t[1]] = t[2]

            @block.sync
            def _(eng):
                run("sp", eng)

            @block.tensor
            def _(eng):
                run("pe", eng)

            @block.scalar
            def _(eng):
                run("act", eng)

            @block.vector
            def _(eng):
                run("dve", eng)

            @block.gpsimd
            def _(eng):
                run("pool", eng)


class Builder:
    def __init__(self, nc, plan):
        self.nc = nc
        self.plan = plan
        self.P = Prog(nc)
        self.res_from_x = True
        self.light_ok = False

    def mm(self, out, lhsT, rhs, start, stop, reads, writes):
        self.P.op("pe", lambda e: e.matmul(out=out, lhsT=lhsT, rhs=rhs, start=start, stop=stop),
                  reads, writes)

    def tr(self, out, in_, ident, reads, writes):
        self.P.op("pe", lambda e: e.transpose(out=out, in_=in_, identity=ident), reads, writes)

    def act(self, out, in_, func, reads, writes, **kw):
        self.P.op("act", lambda e: e.activation(out=out, in_=in_, func=func, **kw), reads, writes)

    def tt(self, eng, out, in0, in1, op, reads, writes):
        self.P.op(eng, lambda e: e.tensor_tensor(out=out, in0=in0, in1=in1, op=op), reads, writes)

    def ts(self, eng, out, in0, s1, s2, op0, op1, reads, writes):
        if s2 is None:
            self.P.op(eng, lambda e: e.tensor_scalar(out=out, in0=in0, scalar1=s1, scalar2=None, op0=op0),
                      reads, writes)
        else:
            self.P.op(eng, lambda e: e.tensor_scalar(out=out, in0=in0, scalar1=s1, scalar2=s2, op0=op0, op1=op1),
                      reads, writes)

    def stt(self, out, in0, scalar, in1, op0, op1, reads, writes):
        self.P.op("dve", lambda e: e.scalar_tensor_tensor(out=out, in0=in0, scalar=scalar, in1=in1,
                                                           op0=op0, op1=op1), reads, writes)

    def cp(self, eng, out, in_, reads, writes):
        if eng == "act":
            self.act(out, in_, AF.Copy, reads, writes)
        else:
            self.P.op(eng, lambda e: e.tensor_copy(out=out, in_=in_), reads, writes)

    def memset(self, eng, ap, val, writes):
        self.P.op(eng, lambda e: e.memset(ap, val), (), writes)

    def carve(self, off, shape, dt):
        n = 1
        for s in shape[1:]:
            n *= s
        esz = 4 if dt in (F32, I32) else 2
        assert off % 4 == 0
        a = self.arena[:, off // 2: off // 2 + n * esz // 2]
        if dt != BF16:
            a = a.bitcast(dt)
        if len(shape) == 2:
            return a
        names = "abcdef"[: len(shape) - 1]
        kw = {names[i]: shape[1 + i] for i in range(len(shape) - 2)}
        return a.rearrange("p (%s) -> p %s" % (" ".join(names), " ".join(names)), **kw)

    def pbank(self, b, dt=F32):
        v = self.psum[:, b, :]
        if dt == BF16:
            v = v.bitcast(BF16)
        return v

    def build(self):
        nc = self.nc
        dr = lambda name, shape, dt, kind: nc.dram_tensor(name, shape, dt, kind=kind).ap()
        I = "ExternalInput"
        self.x = dr("x", [SEQ, D], F32, I)
        self.mem = dr("mem", [NMEM, D], F32, I)
        self.mem_norm_g = dr("mem_norm_g", [D], F32, I)
        self.mix_norm_g = dr("mix_norm_g", [DEPTH, D], F32, I)
        self.xattn_norm_g = dr("xattn_norm_g", [DEPTH, D], F32, I)
        self.mlp_norm_g = dr("mlp_norm_g", [DEPTH, D], F32, I)
        self.s5_a_re = dr("s5_a_re", [2, 64, 64], F32, I)
        self.s5_a_im = dr("s5_a_im", [2, 64, 64], F32, I)
        self.s5_log_dt = dr("s5_log_dt", [2, 64], F32, I)
        self.s5_b_re = dr("s5_b_re", [2, 64, 64, 16], F32, I)
        self.s5_b_im = dr("s5_b_im", [2, 64, 64, 16], F32, I)
        self.s5_c_re = dr("s5_c_re", [2, 64, 16, 64], F32, I)
        self.s5_c_im = dr("s5_c_im", [2, 64, 16, 64], F32, I)
        self.s5_d = dr("s5_d", [2, D], F32, I)
        self.s5_w_glu = dr("s5_w_glu", [2, D, 2 * D], F32, I)
        self.conv_w_in = dr("conv_w_in", [2, D, 3 * D], F32, I)
        self.conv_w = dr("conv_w", [2, 3, D], F32, I)
        self.conv_w_out = dr("conv_w_out", [2, D, D], F32, I)
        self.xa_w_q = dr("xa_w_q", [DEPTH, D, D], F32, I)
        self.xa_w_kv = dr("xa_w_kv", [DEPTH, D, 2 * D], F32, I)
        self.xa_w_o = dr("xa_w_o", [DEPTH, D, D], F32, I)
        self.mlp_w1 = dr("mlp_w1", [DEPTH, D, HID], F32, I)
        self.mlp_w2 = dr("mlp_w2", [DEPTH, HID, D], F32, I)
        self.final_norm_g = dr("final_norm_g", [D], F32, I)
        self.y = dr("y", [SEQ, D], F32, "ExternalOutput")
        self.gy = dr("gy", [SEQ, D], BF16, "Internal")
        self.hts = dr("hts", [NHB, 128, 8 * 512], BF16, "Internal")

        with contextlib.ExitStack() as st:
            ARENA_BYTES = 204 * 1024
            self.arena = st.enter_context(nc.sbuf_tensor("arena", [128, ARENA_BYTES // 2], BF16))
            self.psum = st.enter_context(nc.psum_tensor("psum", [128, 8, 512], F32))
            self._layout()
            self._consts()
            self._mem_prep()
            for item in self.plan:
                kind = item[0]
                if kind == "s5":
                    self.pass_s5(item[1])
                elif kind == "glu":
                    self.pass_glu(item[1])
                elif kind == "conv":
                    self.pass_conv(item[1])
                elif kind == "xattn":
                    self.pass_xattn(item[1])
                elif kind == "mlp":
                    self.pass_mlp(item[1], item[2])
                elif kind == "final":
                    self.pass_final()
                elif kind == "copy":
                    self.pass_copy()
            self.P.emit()

    def _layout(self):
        c = self.carve
        o = 0
        self.ident_bf = c(o, [128, 128], BF16); o += 256
        self.ident_f = c(o, [128, 128], F32); o += 512
        self.ones_bf = c(o, [128, 128], BF16); o += 256
        self.ioff = c(o, [128, 128], F32); o += 512
        self.ones_f = c(o, [128, 128], F32); o += 512
        self.mask_f = c(o, [128, 128], F32); o += 512
        self.gain = c(o, [128, D], F32); o += 4096
        self.memT = c(o, [128, 8, NMEM], BF16); o += 4096
        self.KT = c(o, [128, 8, NMEM], BF16); o += 4096
        self.V = c(o, [128, 2, D], BF16); o += 4096
        self.junk = c(o, [128, D], BF16); o += 2048
        self.ss = c(o, [128, 2, 8], F32); o += 64
        self.rstd = c(o, [128, 2, 8], F32); o += 64
        self.cw = c(o, [128, 8, 3], F32); o += 96
        self.bar = {"act": c(o, [128, 2], F32), "dve": c(o + 8, [128, 2], F32), "pool": c(o + 16, [128, 2], F32)}
        self.bar["src"] = self.ident_f[:, 0:2]
        o += 32
        o = (o + 63) // 64 * 64
        self.o_xb = o; o += 32768
        self.o_r2 = o; o += 82 * 1024
        self.o_ar = o; o += 65536
        assert o <= 204 * 1024, o
        self.XB = [c(self.o_xb + i * 16384, [128, 4, D], F32) for i in range(2)]
        self.XB32 = c(self.o_xb, [128, 8, D], F32)
        r2 = self.o_r2
        self.HB = [c(r2 + i * 8192, [128, 4, D], BF16) for i in range(2)]
        self.hT = [c(r2 + 16384 + i * 8192, [128, 8, 512], BF16) for i in range(2)]
        self.o_r3 = r2 + 32768
        self.STG = [c(self.o_r3 + 34816 + i * 8192, [128, 2048], F32) for i in range(2)]
        self.stg_i = 0
        self.stg_n = 0
        self.stg_call = 0
        self.A = [self.o_ar + i * 16384 for i in range(4)]

    def _consts(self):
        P = self.P
        self.memset("pool", self.ones_f, 1.0, ["ones_f"])
        P.op("pool", lambda e: e.affine_select(out=self.ident_f, in_=self.ones_f, pattern=[[1, 128]],
                                               compare_op=ALU.is_equal, fill=0.0, base=0, channel_multiplier=-1),
             ["ones_f"], ["ident_f"])
        self.cp("pool", self.ident_bf, self.ident_f, ["ident_f"], ["ident_bf"])
        self.cp("pool", self.ones_bf, self.ones_f, ["ones_f"], ["ones_bf"])
        self.memset("pool", self.ioff, 0.0, ["ioff"])
        self.cp("pool", self.ioff[0:64, 64:128], self.ident_f[0:64, 0:64], ["ident_f", "ioff"], ["ioff"])
        self.cp("pool", self.ioff[64:128, 0:64], self.ident_f[64:128, 64:128], ["ident_f", "ioff"], ["ioff"])
        P.op("pool", lambda e: e.affine_select(out=self.mask_f.rearrange("p (a b) -> p a b", a=8),
                                               in_=self.ones_f.rearrange("p (a b) -> p a b", a=8),
                                               pattern=[[16, 8], [0, 16]], compare_op=ALU.is_ge, fill=0.0,
                                               base=15, channel_multiplier=-1),
             ["ones_f"], ["mask_f"])

    def load_gain(self, vec_ap):
        self.P.dma("sp", self.gain, vec_ap.partition_broadcast(128), (), ["gain"])

    def load_w(self, dram_ap, off, K, N, keys):
        kt = K // 128
        view = self.carve(off, [128, kt, N], BF16)
        nch = (N + 2047) // 2048
        cw_ = N // nch
        wkeys = []
        self.stg_call += 1
        for k in range(kt):
            for ci in range(nch):
                src = dram_ap[k * 128:(k + 1) * 128, ci * cw_:(ci + 1) * cw_]
                dst = view[:, k, ci * cw_:(ci + 1) * cw_]
                key = ("Wc", self.stg_call, off, k, ci)
                wkeys.append(key)
                self.stg_n += 1
                if self.stg_n % 2 == 0:
                    self.P.dma("pool", dst, src, (), [key], semkey="W:" + keys[0])
                    continue
                i = self.stg_i % 2
                self.stg_i += 1
                st = self.STG[i][:, 0:cw_]
                self.P.dma("sp", st, src, (), [("stg", i)])
                self.cp("dve" if i == 0 else "act", dst, st, [("stg", i)], [key])
        return view, wkeys

    def src_std(self):
        t = self.x if self.res_from_x else self.y
        return t.rearrange("(hb p s) d -> hb p s d", p=128, s=4)

    def dst_std(self):
        return self.y.rearrange("(hb p s) d -> hb p s d", p=128, s=4)

    def pre_x0(self):
        self.load_xb(0, 0)
        self._skip0 = True

    def load_xb(self, hb, buf):
        if hb == 0 and getattr(self, "_skip0", False):
            self._skip0 = False
            return
        self.P.dma("sp", self.XB[buf], self.src_std()[hb], [("y", hb)], [("XB", buf)])

    def store_xb(self, hb, buf, final=False):
        self.P.dma("sp", self.dst_std()[hb], self.XB[buf], [("XB", buf)], [("y", hb)],
                   semkey=("st", buf), final=final)

    def norm(self, xb, ns, sbuf_i, out_fn, xkeys, okeys, junk_fn):
        ss = self.ss[:, sbuf_i, 0:ns]
        rs = self.rstd[:, sbuf_i, 0:ns]
        kss = [("ss", sbuf_i, s) for s in range(ns)]
        krs = ("rstd", sbuf_i)
        for s in range(ns):
            self.act(junk_fn(s), xb[:, s, :], AF.Square, xkeys, [kss[s]], accum_out=self.ss[:, sbuf_i, s:s + 1])
        self.ts("dve", rs, ss, 1.0 / D, EPS, ALU.mult, ALU.add, kss, [krs])
        self.act(rs, rs, AF.Sqrt, [krs], [krs])
        self.P.op("dve", lambda e: e.reciprocal(out=rs, in_=rs), [krs], [krs])
        for s in range(ns):
            o_ap, i0, i1 = out_fn(s)
            self.stt(o_ap, i0, self.rstd[:, sbuf_i, s:s + 1], i1, ALU.mult, ALU.mult,
                     list(xkeys) + [krs, "gain"], okeys)

    def std_norm(self, buf):
        xb = self.XB[buf]
        hbuf = self.HB[buf]
        self.norm(xb, 4, buf, lambda s: (hbuf[:, s, :], xb[:, s, :], self.gain), [("XB", buf)], [("HB", buf)],
                  lambda s: hbuf[:, s, :])

    def transposes(self, src, ns, dstT, skeys, dkeys, tbanks=(0,), tok_off=0):
        for j in range(8):
            q = tbanks[j % len(tbanks)]
            pt = self.pbank(q, BF16)[:, 0: ns * 128]
            for s in range(ns):
                self.tr(pt[:, s * 128:(s + 1) * 128], src[:, s, j * 128:(j + 1) * 128], self.ident_bf,
                        list(skeys) + ["ident_bf"], [("ps", q)])
            eng = "dve" if j % 2 == 0 else "act"
            self.cp(eng, dstT[:, j, tok_off: tok_off + ns * 128], pt, [("ps", q)], dkeys)

    def residual_out(self, buf, lhs_fn, nk, w_view, lkeys, wkeys, banks):
        xb = self.XB[buf]
        i = 0
        for s in range(4):
            for oh in range(2):
                b = banks[i % len(banks)]
                i += 1
                pb = self.pbank(b)
                for k in range(nk):
                    self.mm(pb, lhs_fn(k, s), w_view[:, k, oh * 512:(oh + 1) * 512], k == 0, k == nk - 1,
                            list(lkeys) + list(wkeys), [("ps", b)])
                xs = xb[:, s, oh * 512:(oh + 1) * 512]
                self.tt("dve", xs, pb, xs, ALU.add, [("ps", b), ("XB", buf)], [("XB", buf)])

    def _mem_prep(self):
        P = self.P
        self.load_gain(self.mem_norm_g)
        xb = self.XB[0]
        P.dma("sp", xb[:, 0:2, :], self.mem.rearrange("(kt p) d -> p kt d", p=128), (), [("XB", 0)])
        hbuf = self.HB[0]
        self.norm(xb, 2, 0, lambda s: (hbuf[:, s, :], xb[:, s, :], self.gain), [("XB", 0)], [("HB", 0)],
                  lambda s: hbuf[:, s, :])
        self.transposes(hbuf, 2, self.memT, [("HB", 0)], ["memT"])
        P.barrier(self.bar)

    def pass_mlp(self, li, hh):
        P = self.P
        P.barrier(self.bar, light=self.light_ok)
        self.load_gain(self.mlp_norm_g[li])
        self.pre_x0()
        w1, w1k = self.load_w(self.mlp_w1[li][:, hh * 2048:(hh + 1) * 2048], self.A[0], D, 2048, ["A0", "A1"])
        w2v = [None, None]
        aT = self.carve(self.o_r3, [128, 16, 512], BF16)
        rl = [self.carve(self.o_r3 + 16384 + i * 2048, [128, 512], F32) for i in range(2)]

        def stageA(hb):
            buf = hb % 2
            self.load_xb(hb, buf)
            hflat = self.hT[buf].rearrange("p a b -> p (a b)")
            if hh == 0:
                self.std_norm(buf)
                self.transposes(self.HB[buf], 4, self.hT[buf], [("HB", buf)], [("hT", buf)], tbanks=(0, 1))
                P.dma("sp", self.hts[hb], hflat, [("hT", buf)], [("hts", hb)], semkey=("hst", buf))
            else:
                P.dma("sp", hflat, self.hts[hb], [("hts", hb)], [("hT", buf)])

        def stageB(hb):
            buf = hb % 2
            for f in range(16):
                b = 2 + (f % 2)
                pb = self.pbank(b)
                for k in range(8):
                    self.mm(pb, w1[:, k, f * 128:(f + 1) * 128], self.hT[buf][:, k, :], k == 0, k == 7,
                            w1k + [("hT", buf)], [("ps", b)])
                r = rl[f % 2]
                self.act(r, pb, AF.Relu, [("ps", b)], [("rl", f % 2)])
                self.tt("dve", aT[:, f, :], r, r, ALU.mult, [("rl", f % 2)], [("aT", f)])

        def stageC(hb):
            buf = hb % 2
            self.residual_out(buf, lambda k, s: aT[:, k, s * 128:(s + 1) * 128], 16, w2v[0],
                              [("aT", f) for f in range(16)], w2v[1], [4, 5, 6, 7])
            self.store_xb(hb, buf)

        stageA(0)
        w2v[0], w2v[1] = self.load_w(self.mlp_w2[li][hh * 2048:(hh + 1) * 2048, :], self.A[2], 2048, D, ["A2", "A3"])
        for hb in range(NHB):
            stageB(hb)
            if hb + 1 < NHB:
                stageA(hb + 1)
            stageC(hb)
        self.res_from_x = False
        self.light_ok = True

    def pass_xattn(self, li):
        P = self.P
        P.barrier(self.bar, light=self.light_ok)
        self.load_gain(self.xattn_norm_g[li])
        self.pre_x0()
        wkv, wkvk = self.load_w(self.xa_w_kv[li], self.A[2], D, 2 * D, ["A2", "A3"])
        wq, wqk = self.load_w(self.xa_w_q[li], self.A[0], D, D, ["A0"])
        wov = [None, None]
        for m in range(8):
            b = 1 + (m % 2)
            pb = self.pbank(b)[:, 0:NMEM]
            for k in range(8):
                self.mm(pb, wkv[:, k, m * 128:(m + 1) * 128], self.memT[:, k, :], k == 0, k == 7,
                        wkvk + ["memT"], [("ps", b)])
            self.cp("act" if m % 2 else "dve", self.KT[:, m, :], pb, [("ps", b)], ["KT"])
        i = 0
        for kt in range(2):
            for oh in range(2):
                b = 3 + (i % 2)
                i += 1
                pb = self.pbank(b)
                for k in range(8):
                    self.mm(pb, self.memT[:, k, kt * 128:(kt + 1) * 128],
                            wkv[:, k, D + oh * 512: D + (oh + 1) * 512], k == 0, k == 7,
                            wkvk + ["memT"], [("ps", b)])
                self.cp("act" if i % 2 else "dve", self.V[:, kt, oh * 512:(oh + 1) * 512], pb, [("ps", b)], ["V"])

        r3 = self.o_r3
        qT = self.carve(r3, [128, 8, 512], BF16)
        oT = self.carve(r3 + 8192, [128, 8, 512], BF16)
        PT = [self.carve(r3 + 16384 + i * 2048, [128, 2, 512], BF16) for i in range(2)]
        rc = [self.carve(r3 + 20480 + i * 2048, [128, 512], F32) for i in range(2)]

        def stageA(hb):
            buf = hb % 2
            self.load_xb(hb, buf)
            self.std_norm(buf)
            self.transposes(self.HB[buf], 4, self.hT[buf], [("HB", buf)], [("hT", buf)])

        def scores(hd):
            par = hd % 2
            for kt in range(2):
                b = 2 + par * 2 + kt
                pb = self.pbank(b)
                for dd in range(2):
                    self.mm(pb, self.KT[:, 2 * hd + dd, kt * 128:(kt + 1) * 128], qT[:, 2 * hd + dd, :],
                            dd == 0, dd == 1, ["KT", ("qT", 2 * hd + dd)], [("ps", b)])
                self.act(PT[par][:, kt, :], pb, AF.Exp, [("ps", b)], [("PT", par, kt)])

        def pv(hd):
            par = hd % 2
            pkeys = [("PT", par, 0), ("PT", par, 1)]
            pbs = self.pbank(6)
            for kt in range(2):
                self.mm(pbs, self.ones_bf, PT[par][:, kt, :], kt == 0, kt == 1, pkeys + ["ones_bf"], [("ps", 6)])
            self.P.op("dve", lambda e: e.reciprocal(out=rc[par], in_=pbs), [("ps", 6)], [("rc", par)])
            for dv in range(2):
                b = 1 if dv == 0 else 7
                pb = self.pbank(b)
                for kt in range(2):
                    self.mm(pb, self.V[:, kt, (2 * hd + dv) * 128:(2 * hd + dv + 1) * 128], PT[par][:, kt, :],
                            kt == 0, kt == 1, pkeys + ["V"], [("ps", b)])
                self.tt("dve", oT[:, 2 * hd + dv, :], pb, rc[par], ALU.mult, [("ps", b), ("rc", par)],
                        [("oT", 2 * hd + dv)])

        def stageB(hb):
            buf = hb % 2
            for m in range(8):
                b = 6 + (m % 2)
                pb = self.pbank(b)
                for k in range(8):
                    self.mm(pb, wq[:, k, m * 128:(m + 1) * 128], self.hT[buf][:, k, :], k == 0, k == 7,
                            wqk + [("hT", buf)], [("ps", b)])
                self.act(qT[:, m, :], pb, AF.Copy, [("ps", b)], [("qT", m)], scale=1.0 / 16.0)
            scores(0)
            scores(1)
            pv(0)
            scores(2)
            pv(1)
            scores(3)
            pv(2)
            pv(3)

        def stageC(hb):
            buf = hb % 2
            self.residual_out(buf, lambda k, s: oT[:, k, s * 128:(s + 1) * 128], 8, wov[0],
                              [("oT", m) for m in range(8)], wov[1], [2, 3, 4, 5])
            self.store_xb(hb, buf)

        stageA(0)
        wov[0], wov[1] = self.load_w(self.xa_w_o[li], self.A[1], D, D, ["A1"])
        for hb in range(NHB):
            stageB(hb)
            if hb + 1 < NHB:
                stageA(hb + 1)
            stageC(hb)
        self.res_from_x = False
        self.light_ok = True

    def pass_conv(self, li):
        P = self.P
        j = li // 2
        P.barrier(self.bar, light=self.light_ok)
        self.load_gain(self.mix_norm_g[li])
        self.pre_x0()
        w_in, w_ink = self.load_w(self.conv_w_in[j], self.A[0], D, 3 * D, ["A0", "A1", "A2"])
        woutv = [None, None]
        with self.nc.allow_non_contiguous_dma(reason="tiny conv taps"):
            for k in range(3):
                P.dma("sp", self.cw[:, :, k], self.conv_w[j][k].rearrange("(m p) -> p m", p=128), (), ["cw"], allow_slow_non_contiguous=True)
        r3 = self.o_r3
        Z = self.carve(r3, [128, 8, 4, 132], F32)
        cs = [self.carve(r3 + 16896 + i * 2048, [128, 4, 128], F32) for i in range(2)]
        zc = [self.carve(r3 + 20992 + i * 2048, [128, 4, 128], F32) for i in range(2)]
        gT = self.carve(r3 + 25088, [128, 8, 512], BF16)
        self.memset("pool", Z, 0.0, [("Z", m) for m in range(8)])

        def stageA(hb):
            buf = hb % 2
            self.load_xb(hb, buf)
            self.std_norm(buf)
            self.transposes(self.HB[buf], 4, self.hT[buf], [("HB", buf)], [("hT", buf)])

        def stageB(hb):
            buf = hb % 2
            for m in range(8):
                par = m % 2
                bb, bc, bv = 1 + par * 3, 2 + par * 3, 3 + par * 3
                for X, b in ((1, bc), (2, bv), (0, bb)):
                    pb = self.pbank(b)
                    for k in range(8):
                        self.mm(pb, w_in[:, k, X * D + m * 128: X * D + (m + 1) * 128], self.hT[buf][:, k, :],
                                k == 0, k == 7, w_ink + [("hT", buf)], [("ps", b)])
                c3 = cs[par]
                z3 = zc[par]
                kz = ("Z", m)
                self.act(c3.rearrange("p a b -> p (a b)"), self.pbank(bc), AF.Copy, [("ps", bc)], [("cs", par)])
                Zm = Z[:, m, :, :]
                self.tt("dve", Zm[:, :, 1:129], self.pbank(bv).rearrange("p (a b) -> p a b", a=4), c3, ALU.mult,
                        [("ps", bv), ("cs", par)], [kz])
                w0 = self.cw[:, m, 0:1]
                w1 = self.cw[:, m, 1:2]
                w2 = self.cw[:, m, 2:3]
                kzc = ("zc", par)
                self.ts("dve", z3, Zm[:, :, 1:129], w2, None, ALU.mult, None, [kz, "cw"], [kzc])
                self.stt(z3[:, 1:4, :], Zm[:, 0:3, 1:129], w1, z3[:, 1:4, :], ALU.mult, ALU.add, [kz, kzc, "cw"], [kzc])
                self.stt(z3[:, 0, :], Zm[:, 3, 0:128], w1, z3[:, 0, :], ALU.mult, ALU.add, [kz, kzc, "cw"], [kzc])
                self.stt(z3[:, 2:4, :], Zm[:, 0:2, 1:129], w0, z3[:, 2:4, :], ALU.mult, ALU.add, [kz, kzc, "cw"], [kzc])
                self.stt(z3[:, 1, :], Zm[:, 3, 0:128], w0, z3[:, 1, :], ALU.mult, ALU.add, [kz, kzc, "cw"], [kzc])
                self.stt(z3[:, 0, :], Zm[:, 2, 0:128], w0, z3[:, 0, :], ALU.mult, ALU.add, [kz, kzc, "cw"], [kzc])
                self.tt("dve", gT[:, m, :], self.pbank(bb), z3.rearrange("p a b -> p (a b)"), ALU.mult,
                        [("ps", bb), kzc], [("gT", m)])
                self.cp("pool", Zm[:, 2:4, 0:1], Zm[:, 2:4, 128:129], [kz], [kz])

        def stageC(hb):
            buf = hb % 2
            self.residual_out(buf, lambda k, s: gT[:, k, s * 128:(s + 1) * 128], 8, woutv[0],
                              [("gT", m) for m in range(8)], woutv[1], [7, 1, 2, 3])
            self.store_xb(hb, buf)

        stageA(0)
        woutv[0], woutv[1] = self.load_w(self.conv_w_out[li // 2], self.A[3], D, D, ["A3"])
        for hb in range(NHB):
            stageB(hb)
            if hb + 1 < NHB:
                stageA(hb + 1)
            stageC(hb)
        self.res_from_x = False
        self.light_ok = True

    def pass_glu(self, li):
        P = self.P
        j = li // 2
        P.barrier(self.bar)
        self.pre_x0()
        wg, wgk = self.load_w(self.s5_w_glu[j], self.A[0], D, 2 * D, ["A0", "A1"])
        gyv = self.gy.rearrange("(hb p s) d -> hb p s d", p=128, s=4)
        r3 = self.o_r3
        sg = [self.carve(r3 + i * 2048, [128, 512], F32) for i in range(2)]
        tm = [self.carve(r3 + 4096 + i * 2048, [128, 512], F32) for i in range(2)]

        def stageA(hb):
            buf = hb % 2
            self.load_xb(hb, buf)
            P.dma("sp", self.HB[buf], gyv[hb], [("gy", hb)], [("HB", buf)])
            self.transposes(self.HB[buf], 4, self.hT[buf], [("HB", buf)], [("hT", buf)], tbanks=(0, 7))

        def stageC(hb):
            buf = hb % 2
            xb = self.XB[buf]
            for s in range(4):
                for oh in range(2):
                    par = oh
                    bv, bg = [(1, 2), (3, 4), (5, 6)][(s * 2 + oh) % 3]
                    for b, col in ((bv, oh * 512), (bg, D + oh * 512)):
                        pb = self.pbank(b)
                        for k in range(8):
                            self.mm(pb, self.hT[buf][:, k, s * 128:(s + 1) * 128], wg[:, k, col: col + 512],
                                    k == 0, k == 7, wgk + [("hT", buf)], [("ps", b)])
                    self.act(sg[par], self.pbank(bg), AF.Sigmoid, [("ps", bg)], [("sg", par)])
                    self.tt("dve", tm[par], self.pbank(bv), sg[par], ALU.mult, [("ps", bv), ("sg", par)], [("tm", par)])
                    xs = xb[:, s, oh * 512:(oh + 1) * 512]
                    self.tt("dve", xs, xs, tm[par], ALU.add, [("tm", par), ("XB", buf)], [("XB", buf)])
            self.store_xb(hb, buf)

        stageA(0)
        for hb in range(NHB):
            if hb + 1 < NHB:
                stageA(hb + 1)
            stageC(hb)
        self.res_from_x = False
        self.light_ok = True

    def pass_final(self):
        P = self.P
        P.barrier(self.bar, light=self.light_ok)
        self.load_gain(self.final_norm_g)
        for hb in range(NHB):
            buf = hb % 2
            self.load_xb(hb, buf)
            xb = self.XB[buf]
            hbuf = self.HB[buf]
            self.norm(xb, 4, buf, lambda s: (xb[:, s, :], xb[:, s, :], self.gain), [("XB", buf)], [("XB", buf)],
                      lambda s, hbuf=hbuf: hbuf[:, s, :])
            self.store_xb(hb, buf, final=True)
        self.res_from_x = False

    def pass_copy(self):
        self.P.barrier(self.bar)
        for hb in range(NHB):
            buf = hb % 2
            self.load_xb(hb, buf)
            self.store_xb(hb, buf, final=True)
        self.res_from_x = False

    def pass_s5(self, li):
        P = self.P
        nc = self.nc
        j = li // 2
        P.barrier(self.bar)
        self.light_ok = False
        self.load_gain(self.mix_norm_g[li])
        c = self.carve
        NK = 34
        r2 = self.o_r2
        T0T = c(r2, [128, 64, 128], BF16)
        BtT = c(r2 + 16384, [128, 64, 128], BF16)
        CtT = c(r2 + 32768, [128, 64, 128], BF16)
        GYt = [c(r2 + 49152 + i * 8192, [128, 4, 8, 128], BF16) for i in range(2)]
        U = [c(r2 + 65536 + i * 1024, [128, 512], BF16) for i in range(4)]
        XS = [c(r2 + 69632 + i * 1032, [128, 516], BF16) for i in range(4)]
        MR = [c(r2 + 73760 + i * 256, [128, 128], BF16) for i in range(18)]
        v1 = c(r2 + 78368, [128, 64, 9], F32)
        v2 = c(r2 + 80672, [128, 64, 9], F32)
        assert 80672 + 2304 <= 82 * 1024
        a = self.o_ar
        are = c(a, [128, 64], F32)
        aim = c(a + 256, [128, 64], F32)
        dtb = c(a + 512, [128, 64], F32)
        ktab = c(a + 768, [128, NK], F32)
        dcol = c(a + 1024, [128, 64], F32)
        ldr = c(a + 1280, [128, 64], F32)
        ldi = c(a + 1536, [128, 64], F32)
        fr = c(a + 1792, [128, 64], F32)
        fi = c(a + 2048, [128, 64], F32)
        t1 = c(a + 2304, [128, 64], F32)
        t2 = c(a + 2560, [128, 64], F32)
        o0 = a + 3072
        TB = 8704
        MAG = c(o0, [128, 64, NK], F32)
        U1 = c(o0 + TB, [128, 64, NK], F32)
        U2 = c(o0 + 2 * TB, [128, 64, NK], F32)
        o1 = o0 + 3 * TB
        Cre = c(o1, [128, 64, 16], F32)
        Cim = c(o1 + 4096, [128, 64, 16], F32)
        Bre = c(o1 + 8192, [128, 64, 16], F32)
        Bim = c(o1 + 12288, [128, 64, 16], F32)
        F1 = c(o1 + 16384, [128, 64, 9], F32)
        F2 = c(o1 + 16384 + 2304, [128, 64, 9], F32)
        G1 = c(o1 + 16384 + 4608, [128, 64, 16], F32)
        G2 = c(o1 + 16384 + 4608 + 4096, [128, 64, 16], F32)
        assert o1 + 16384 + 4608 + 8192 <= a + 65536
        xo = self.o_xb
        TI = c(xo, [128, 64 * NK], I32)
        TF = c(xo + TB, [128, 64 * NK], F32)
        AL = c(xo + 2 * TB, [128, 2, 128], F32)[0:64]
        CL = c(xo + 2 * TB + 1024, [128, 2, 8, 128], F32)
        ERB = c(xo + 2 * TB + 1024 + 8192, [128, 64, 16], F32)
        EIB = c(xo + 2 * TB + 1024 + 12288, [128, 64, 16], F32)
        assert 2 * TB + 1024 + 16384 <= 32768 + 4096
        WC = c(xo, [128, 8, 9, 16], F32)
        WCt = c(xo + 4608, [128, 8, 9, 16], F32)
        WB = c(xo + 9216, [128, 8, 16, 16], F32)
        WBt = c(xo + 17408, [128, 8, 16, 16], F32)
        T0s = [c(xo + 25600 + i * 512, [128, 128], F32) for i in range(2)]
        MT1 = [c(xo + 26624 + i * 512, [128, 128], F32) for i in range(2)]
        MT2 = [c(xo + 27648 + i * 512, [128, 128], F32) for i in range(2)]

        klist = list(range(0, 9)) + [-s for s in range(8)] + [7 - s for s in range(8)] + [8 * 2 ** l for l in range(9)]
        assert len(klist) == NK
        for i, kv in enumerate(klist):
            self.memset("pool", ktab[:, i:i + 1], float(kv), ["ktab"])
        for h in range(2):
            P.dma("sp", AL[:, 0, h * 64:(h + 1) * 64], self.s5_a_re[j], (), ["AL"])
            P.dma("sp", AL[:, 1, h * 64:(h + 1) * 64], self.s5_a_im[j], (), ["AL"])
        P.dma("sp", dtb, self.s5_log_dt[j].partition_broadcast(128), (), ["dtb"])
        for t, dst in ((0, are), (1, aim)):
            pb = self.pbank(1 + t)[:, 0:64]
            self.tr(pb, AL[:, t, :], self.ident_f[0:64, 0:64], ["AL", "ident_f"], [("ps", 1 + t)])
            self.cp("dve", dst, pb, [("ps", 1 + t)], ["a%d" % t])
        self.act(dtb, dtb, AF.Exp, ["dtb"], ["dtb"])
        self.tt("dve", ldr, are, dtb, ALU.mult, ["a0", "dtb"], ["ld0"])
        self.tt("dve", ldi, aim, dtb, ALU.mult, ["a1", "dtb"], ["ld1"])
        kb = ktab.unsqueeze(1).to_broadcast([128, 64, NK])
        self.tt("dve", MAG, ldr.unsqueeze(2).to_broadcast([128, 64, NK]), kb, ALU.mult, ["ld0", "ktab"], ["MAG"])
        self.tt("dve", U1, ldi.unsqueeze(2).to_broadcast([128, 64, NK]), kb, ALU.mult, ["ld1", "ktab"], ["U1"])
        self.act(MAG, MAG, AF.Exp, ["MAG"], ["MAG"])
        U1f = U1.rearrange("p a b -> p (a b)")
        U2f = U2.rearrange("p a b -> p (a b)")
        MAGf = MAG.rearrange("p a b -> p (a b)")
        self.ts("dve", U1f, U1f, 1.0 / (2 * math.pi), None, ALU.mult, None, ["U1"], ["U1"])
        self.ts("dve", U2f, U1f, 0.25, None, ALU.add, None, ["U1"], ["U2"])
        for Ux, kx in ((U1f, "U1"), (U2f, "U2")):
            self.cp("dve", TI, Ux, [kx], ["TI"])
            self.cp("dve", TF, TI, ["TI"], ["TF"])
            self.tt("dve", Ux, Ux, TF, ALU.subtract, [kx, "TF"], [kx])
            self.ts("dve", TF, Ux, 0.5, None, ALU.is_ge, None, [kx], ["TF"])
            self.tt("dve", Ux, Ux, TF, ALU.subtract, [kx, "TF"], [kx])
            self.ts("dve", TF, Ux, -0.5, None, ALU.is_lt, None, [kx], ["TF"])
            self.tt("dve", Ux, Ux, TF, ALU.add, [kx, "TF"], [kx])
            self.ts("dve", Ux, Ux, 2 * math.pi, None, ALU.mult, None, [kx], [kx])
            self.act(Ux, Ux, AF.Sin, [kx], [kx])
        self.tt("dve", U1f, U1f, MAGf, ALU.mult, ["U1", "MAG"], ["U1"])
        self.tt("dve", U2f, U2f, MAGf, ALU.mult, ["U2", "MAG"], ["U2"])
        EIM, ERE = U1, U2
        NEI = MAG
        self.ts("dve", MAGf, U1f, -1.0, None, ALU.mult, None, ["U1", "MAG"], ["MAG"])
        nr, ni = t1, t2
        self.ts("dve", nr, ERE[:, :, 1], -1.0, None, ALU.add, None, ["U2"], ["nr"])
        self.cp("dve", ni, EIM[:, :, 1], ["U1"], ["ni"])
        den = fi
        self.tt("dve", den, are, are, ALU.mult, ["a0"], ["den"])
        self.tt("dve", fr, aim, aim, ALU.mult, ["a1"], ["fr"])
        self.tt("dve", den, den, fr, ALU.add, ["den", "fr"], ["den"])
        self.P.op("dve", lambda e: e.reciprocal(out=dtb, in_=den), ["den", "dtb"], ["rden"])
        rden = dtb
        self.tt("dve", fr, nr, are, ALU.mult, ["nr", "a0", "fr"], ["fr"])
        self.tt("dve", ldr, ni, aim, ALU.mult, ["ni", "a1", "ld0", "MAG"], ["ld0"])
        self.tt("dve", fr, fr, ldr, ALU.add, ["fr", "ld0"], ["fr"])
        self.tt("dve", fr, fr, rden, ALU.mult, ["fr", "rden"], ["fr"])
        self.tt("dve", fi, ni, are, ALU.mult, ["ni", "a0", "den", "rden"], ["fi"])
        self.tt("dve", ldr, nr, aim, ALU.mult, ["nr", "a1", "ld0", "fr"], ["ld0"])
        self.tt("dve", fi, fi, ldr, ALU.subtract, ["fi", "ld0"], ["fi"])
        self.tt("dve", fi, fi, rden, ALU.mult, ["fi", "rden"], ["fi"])
        b3 = [128, 64, 16]
        frb = fr.unsqueeze(2).to_broadcast(b3)
        fib = fi.unsqueeze(2).to_broadcast(b3)
        self.tt("dve", ERB, ERE[:, :, 9:25], frb, ALU.mult, ["U2", "fr"], ["ERB"])
        self.tt("dve", G1, EIM[:, :, 9:25], fib, ALU.mult, ["U1", "fi"], ["G1"])
        self.tt("dve", ERB, ERB, G1, ALU.subtract, ["ERB", "G1"], ["ERB"])
        self.tt("dve", EIB, ERE[:, :, 9:25], fib, ALU.mult, ["U2", "fi"], ["EIB"])
        self.tt("dve", G2, EIM[:, :, 9:25], frb, ALU.mult, ["U1", "fr", "G1"], ["G2"])
        self.tt("dve", EIB, EIB, G2, ALU.add, ["EIB", "G2"], ["EIB"])
        lo, hi = slice(0, 64), slice(64, 128)
        self.cp("pool", F1[lo], ERE[lo, :, 0:9], ["U2"], ["F1"])
        self.cp("pool", F1[hi], NEI[hi, :, 0:9], ["MAG"], ["F1"])
        self.cp("pool", F2[lo], NEI[lo, :, 0:9], ["MAG"], ["F2"])
        self.ts("pool", F2[hi], ERE[hi, :, 0:9], -1.0, None, ALU.mult, None, ["U2"], ["F2"])
        self.cp("pool", G1[lo], ERB[lo], ["ERB", "G1", "EIB"], ["G1"])
        self.cp("pool", G1[hi], EIB[hi], ["EIB", "G1"], ["G1"])
        self.ts("pool", G2[lo], EIB[lo], -1.0, None, ALU.mult, None, ["EIB", "G2"], ["G2"])
        self.cp("pool", G2[hi], ERB[hi], ["ERB", "G2"], ["G2"])
        self.cp("pool", v1, ERE[:, :, 25:34], ["U2"], ["v1"])
        self.cp("pool", v2[lo], EIM[lo, :, 25:34], ["U1"], ["v2"])
        self.cp("pool", v2[hi], NEI[hi, :, 25:34], ["MAG"], ["v2"])
        for t, src in ((0, self.s5_c_re), (1, self.s5_c_im)):
            sv = src[j].rearrange("(gt gl) h p -> (gl h) gt p", gl=8)
            for h in range(2):
                P.dma("sp", CL[:, t, :, h * 64:(h + 1) * 64], sv, (), [("CL", t)])
        for t, dst in ((0, Cre), (1, Cim)):
            for gt in range(8):
                b = 3 + (gt % 2)
                pb = self.pbank(b)[:, 0:128]
                self.tr(pb, CL[:, t, gt, :], self.ident_f, [("CL", t), "ident_f"], [("ps", b)])
                self.cp("dve" if gt % 2 else "act", dst[:, gt * 8:(gt + 1) * 8, :].rearrange("p a b -> p (a b)"), pb,
                        [("ps", b)], ["C%d" % t])
        with nc.allow_non_contiguous_dma(reason="small s5 params"):
            for t, (src, dst) in enumerate(((self.s5_b_re, Bre), (self.s5_b_im, Bim))):
                sv = src[j].rearrange("g p h -> p g h")
                for h in range(2):
                    P.dma("sp", dst[h * 64:(h + 1) * 64], sv, (), ["B%d" % t], allow_slow_non_contiguous=True)
            dv = self.s5_d[j].rearrange("(g h) -> h g", h=16)
            for s in range(8):
                P.dma("sp", dcol[s * 16:(s + 1) * 16, :], dv, (), ["dcol"], allow_slow_non_contiguous=True)
        P.barrier(self.bar)
        for gb in range(8):
            g0 = gb * 8
            gs = slice(g0, g0 + 8)
            b4c = [128, 8, 9, 16]
            b4b = [128, 8, 16, 16]
            self.tt("dve", WC, Cre[:, gs, :].unsqueeze(2).to_broadcast(b4c), F1[:, gs, :].unsqueeze(3).to_broadcast(b4c),
                    ALU.mult, ["C0", "F1"], ["WC"])
            self.tt("dve", WCt, Cim[:, gs, :].unsqueeze(2).to_broadcast(b4c), F2[:, gs, :].unsqueeze(3).to_broadcast(b4c),
                    ALU.mult, ["C1", "F2"], ["WCt"])
            self.tt("dve", WC, WC, WCt, ALU.add, ["WC", "WCt"], ["WC"])
            self.tt("dve", WB, Bre[:, gs, :].unsqueeze(2).to_broadcast(b4b), G1[:, gs, :].unsqueeze(3).to_broadcast(b4b),
                    ALU.mult, ["B0", "G1"], ["WB"])
            self.tt("dve", WBt, Bim[:, gs, :].unsqueeze(2).to_broadcast(b4b), G2[:, gs, :].unsqueeze(3).to_broadcast(b4b),
                    ALU.mult, ["B1", "G2"], ["WBt"])
            self.tt("dve", WB, WB, WBt, ALU.add, ["WB", "WBt"], ["WB"])
            self.cp("act", CtT[:, gs, :].rearrange("p g (s h) -> p g s h", s=8), WC[:, :, 1:9, :], ["WC"], ["CtT"])
            for gl in range(8):
                g = g0 + gl
                par = gl % 2
                pb = self.pbank(1 + par)[:, 0:128]
                self.mm(pb, WB[:, gl, 0:8, :].rearrange("p a b -> p (a b)"),
                        WC[:, gl, 0:8, :].rearrange("p a b -> p (a b)"), True, True, ["WB", "WC"], [("ps", 1 + par)])
                self.tt("dve", T0s[par], pb, self.mask_f, ALU.mult, [("ps", 1 + par), "mask_f"], [("T0s", par)])
                self.stt(T0T[:, g, :], self.ident_f, dcol[:, g:g + 1], T0s[par], ALU.mult, ALU.add,
                         [("T0s", par), "dcol", "ident_f"], ["T0T"])
                pb2 = self.pbank(3 + par)[:, 0:128]
                self.tr(pb2, WB[:, gl, 8:16, :].rearrange("p a b -> p (a b)"), self.ident_f, ["WB", "ident_f"],
                        [("ps", 3 + par)])
                self.cp("act", BtT[:, g, :], pb2, [("ps", 3 + par)], ["BtT"])
        P.barrier(self.bar)

        Hc = [c(self.o_ar + r * 16384, [128, 64, 8, 16], BF16) for r in range(4)]
        src = (self.x if self.res_from_x else self.y).rearrange("(r p s) d -> r p s d", p=128, s=8)
        xb = self.XB32
        g3 = self.gain.rearrange("p (g h) -> p g h", h=16)
        for r in range(4):
            P.dma("sp", xb[:, 0:4, :], src[r][:, 0:4, :], [("y", 2 * r), ("y", 2 * r + 1)], ["XB32"])
            P.dma("sp", xb[:, 4:8, :], src[r][:, 4:8, :], [("y", 2 * r), ("y", 2 * r + 1)], ["XB32b"])
            self.norm(xb, 8, 0,
                      lambda s, r=r: (Hc[r][:, :, s, :], xb[:, s, :].rearrange("p (g h) -> p g h", h=16), g3),
                      ["XB32", "XB32b"], [("Hc", r)],
                      lambda s, r=r: Hc[r].rearrange("p g s h -> p (g s h)")[:, s * 1024:(s + 1) * 1024])
        for q in range(4):
            self.memset("pool", XS[q][:, 0:1], 0.0, [("XS", q)])
        P.barrier(self.bar)
        Mt1 = c(xo, [128, 9, 128], F32)
        Mt2 = c(xo + 4608, [128, 9, 128], F32)
        MRq = [c(xo + 9216 + i * 2304, [128, 9, 128], BF16) for i in range(4)]

        gyw = self.gy.rearrange("(r p s) d -> p r s d", p=128, s=8)
        for gb4 in range(16):
            groups = [gb4 * 4 + q for q in range(4)]
            for q, g in enumerate(groups):
                ub = U[q]
                sl = g % 2
                pt = self.pbank(7 * sl, BF16)[:, 0:512]
                for r in range(4):
                    self.tr(pt[:, r * 128:(r + 1) * 128], Hc[r][:, g, :, :].rearrange("p a b -> p (a b)"),
                            self.ident_bf, [("Hc", r), "ident_bf"], [("ps", 7 * sl)])
                self.cp("act", ub, pt, [("ps", 7 * sl)], [("U", q)])
                b = 1 + q
                pb = self.pbank(b)
                self.mm(pb, BtT[:, g, :], ub, True, True, ["BtT", ("U", q)], [("ps", b)])
                self.cp("act", XS[q][:, 1:513], pb, [("ps", b)], [("XS", q)])
            for lev in range(9):
                d = 1 << lev
                for q, g in enumerate(groups):
                    m1 = Mt1[:, q % 2, :]
                    m2 = Mt2[:, q % 2, :]
                    self.act(m1, self.ident_f, AF.Copy, ["ident_f", "v1"], [("MT1", q % 2)], scale=v1[:, g, lev:lev + 1])
                    self.ts("dve", m2, self.ioff, v2[:, g, lev:lev + 1], None, ALU.mult, None, ["ioff", "v2"], [("MT2", q % 2)])
                    self.tt("pool", MRq[q][:, lev, :], m1, m2, ALU.add, [("MT1", q % 2), ("MT2", q % 2)], [("MRq", q, lev)])
                    b = 1 + q
                    pb = self.pbank(b)[:, 0:512 - d]
                    self.mm(pb, MRq[q][:, lev, :], XS[q][:, 1:513 - d], True, True, [("MRq", q, lev), ("XS", q)], [("ps", b)])
                    xs_hi = XS[q][:, 1 + d:513]
                    self.tt("dve", xs_hi, xs_hi, pb, ALU.add, [("ps", b), ("XS", q)], [("XS", q)])
            for q, g in enumerate(groups):
                gl = g % 8
                jt = g // 8
                jb = jt % 2
                yb = 5 + (g % 2)
                pby = self.pbank(yb)
                for r in range(4):
                    o_ap = pby[:, r * 128:(r + 1) * 128]
                    self.mm(o_ap, U[q][:, r * 128:(r + 1) * 128], T0T[:, g, :], True, False,
                            [("U", q), "T0T"], [("ps", yb)])
                    self.mm(o_ap, XS[q][:, r * 128:(r + 1) * 128], CtT[:, g, :], False, True,
                            [("XS", q), "CtT"], [("ps", yb)])
                for r in range(4):
                    self.act(GYt[jb][:, r, :, gl * 16:(gl + 1) * 16],
                             pby[:, r * 128:(r + 1) * 128].rearrange("p (s h) -> p s h", h=16),
                             AF.Gelu_apprx_tanh, [("ps", yb)], [("GYt", jb)])
                if gl == 7:
                    for r in range(4):
                        P.dma("sp", gyw[:, r, :, jt * 128:(jt + 1) * 128], GYt[jb][:, r, :, :], [("GYt", jb)],
                              [("gy", hh) for hh in range(8)], semkey=("gyst", jb))


FULL_PLAN = []
for _li in range(DEPTH):
    if _li % 2 == 0:
        FULL_PLAN += [("s5", _li), ("glu", _li)]
    else:
        FULL_PLAN += [("conv", _li)]
    FULL_PLAN += [("xattn", _li), ("mlp", _li, 0), ("mlp", _li, 1)]
FULL_PLAN += [("final",)]

WEIGHT_NAMES = ["mem_norm_g", "mix_norm_g", "xattn_norm_g", "mlp_norm_g", "s5_a_re", "s5_a_im", "s5_log_dt",
                "s5_b_re", "s5_b_im", "s5_c_re", "s5_c_im", "s5_d", "s5_w_glu", "conv_w_in", "conv_w",
                "conv_w_out", "xa_w_q", "xa_w_kv", "xa_w_o", "mlp_w1", "mlp_w2", "final_norm_g"]


def build_nc(plan):
    nc = bass.Bass("TRN2", target_bir_lowering=False)
    Builder(nc, plan).build()
    return nc


def run_plan(plan, inputs, n_cores=8, trace=False):
    nc = build_nc(plan)
    shared = {k: np.ascontiguousarray(np.asarray(inputs[k], dtype=np.float32)) for k in WEIGHT_NAMES}
    x = np.asarray(inputs["x"], dtype=np.float32)
    mem = np.asarray(inputs["mem"], dtype=np.float32)
    in_maps = []
    for b in range(n_cores):
        m = dict(shared)
        m["x"] = np.ascontiguousarray(x[b])
        m["mem"] = np.ascontiguousarray(mem[b])
        in_maps.append(m)
    res = run_bass_kernel_spmd(nc, in_maps, core_ids=list(range(n_cores)), trace=trace)
    out = np.stack([np.asarray(r["y"]) for r in res.results], axis=0)
    return out, res


def kernel(**inputs):
    out, _ = run_plan(FULL_PLAN, inputs, n_cores=8)
    return out.astype(np.float32)
```

```python
import contextlib
import math
import numpy as np
import concourse.bass as bass
import concourse.mybir as mybir
from concourse.bass_utils import run_bass_kernel_spmd

F32 = mybir.dt.float32
BF16 = mybir.dt.bfloat16
I32 = mybir.dt.int32
AF = mybir.ActivationFunctionType
ALU = mybir.AluOpType

D = 1024
SEQ = 4096
NMEM = 256
DEPTH = 4
HID = 4096
EPS = 1e-6
NHB = 8
ENGS = ["pe", "act", "dve", "pool", "sp"]


class _Ins:
    __slots__ = ("fn", "deps", "inc", "dma_sem", "dma_cnt", "eng", "idx")

    def __init__(self, fn, eng, idx):
        self.fn = fn
        self.deps = []
        self.inc = False
        self.dma_sem = None
        self.dma_cnt = 0
        self.eng = eng
        self.idx = idx


class Prog:
    def __init__(self, nc):
        self.nc = nc
        self.ins = {e: [] for e in ENGS}
        self.last_w = {}
        self.readers = {}
        self.dma_counts = {}
        self.final_tokens = []
        self.pending = {e: [] for e in ENGS}
        self.pending_c = {e: [] for e in ENGS}

    def _collect(self, eng, reads, writes, is_dma):
        deps = []
        for r in reads:
            t = self.last_w.get(r)
            if t is not None:
                deps.append(("raw", t))
        for w in writes:
            t = self.last_w.get(w)
            if t is not None:
                deps.append(("waw", t))
            for t in self.readers.get(w, ()):
                deps.append(("war", t))
        out = []
        for kind, t in deps:
            if t[0] == "c" and t[1] == eng:
                if eng == "pe":
                    continue
            out.append(t)
        if self.pending[eng]:
            out.extend(self.pending[eng])
            self.pending[eng] = []
        if self.pending_c[eng] and not is_dma:
            out.extend(self.pending_c[eng])
            self.pending_c[eng] = []
        return out

    def _register(self, tok, reads, writes):
        for w in writes:
            self.last_w[w] = tok
            self.readers[w] = []
        for r in reads:
            self.readers.setdefault(r, []).append(tok)

    def op(self, eng, fn, reads=(), writes=()):
        lst = self.ins[eng]
        ins = _Ins(fn, eng, len(lst))
        ins.deps = self._collect(eng, reads, writes, False)
        lst.append(ins)
        tok = ("c", eng, ins.idx)
        self._register(tok, reads, writes)
        return tok

    def dma(self, eng, out, in_, reads=(), writes=(), semkey=None, final=False, **kw):
        if semkey is None:
            semkey = writes[0]
        lst = self.ins[eng]

        def fn(e, out=out, in_=in_, kw=kw):
            return e.dma_start(out=out, in_=in_, **kw)

        ins = _Ins(fn, eng, len(lst))
        ins.deps = self._collect(eng, reads, writes, True)
        cnt = self.dma_counts.get(semkey, 0) + 16
        self.dma_counts[semkey] = cnt
        ins.dma_sem = semkey
        ins.dma_cnt = cnt
        lst.append(ins)
        tok = ("d", semkey, cnt)
        self._register(tok, reads, writes)
        if final:
            self.final_tokens.append(tok)
        return tok

    def barrier(self, bar_aps, light=False):
        toks = []
        for e in ("act", "dve", "pool"):
            ap = bar_aps[e]
            if e == "act":
                src = bar_aps["src"]
                t = self.op(e, lambda en, ap=ap, src=src: en.activation(out=ap, in_=src, func=AF.Copy), ["ident_f"])
            else:
                t = self.op(e, lambda en, ap=ap: en.memset(ap, 0.0))
            toks.append(t)
        if self.ins["pe"]:
            toks.append(("c", "pe", len(self.ins["pe"]) - 1))
        if light:
            for e in ("pe", "act", "dve"):
                self.pending_c[e] = self.pending_c[e] + list(toks)
            self.pending["pool"] = self.pending["pool"] + list(toks)
            return
        for k, c in self.dma_counts.items():
            toks.append(("d", k, c))
        for e in ENGS:
            self.pending[e] = list(toks)

    def emit(self):
        nc = self.nc
        for e in ENGS:
            for ins in self.ins[e]:
                for t in ins.deps:
                    if t[0] == "c":
                        self.ins[t[1]][t[2]].inc = True
        rank = {}
        for e in ENGS:
            c = 0
            for ins in self.ins[e]:
                if ins.inc and ins.dma_sem is None:
                    c += 1
                    rank[(e, ins.idx)] = c
        with contextlib.ExitStack() as st:
            esem = {e: st.enter_context(nc.semaphore("prog_" + e)) for e in ENGS}
            dsem = {}
            for i, k in enumerate(self.dma_counts):
                dsem[k] = st.enter_context(nc.semaphore("dma_%d" % i))
            block = st.enter_context(nc.Block())

            def run(engname, eng):
                known_c = {}
                known_d = {}
                for ins in self.ins[engname]:
                    need_c = {}
                    need_d = {}
                    for t in ins.deps:
                        if t[0] == "c":
                            v = rank[(t[1], t[2])]
                            if v > known_c.get(t[1], 0):
                                need_c[t[1]] = max(need_c.get(t[1], 0), v)
                        else:
                            if t[2] > known_d.get(t[1], 0):
                                need_d[t[1]] = max(need_d.get(t[1], 0), t[2])
                    for e2, v in need_c.items():
                        eng.wait_ge(esem[e2], v)
                        known_c[e2] = v
                    for k, v in need_d.items():
                        eng.wait_ge(dsem[k], v)
                        known_d[k] = v
                    bi = ins.fn(eng)
                    if ins.dma_sem is not None:
                        bi.then_inc(dsem[ins.dma_sem], 16)
                    elif ins.inc:
                        bi.then_inc(esem[engname], 1)
                if engname == "sp":
                    for t in self.final_tokens:
                        if t[2] > known_d.get(t[1], 0):
                            eng.wait_ge(dsem[t[1]], t[2])
                            known_d[t[1]] = t[2]

            @block.sync
            def _(eng):
                run("sp", eng)

            @block.tensor
            def _(eng):
                run("pe", eng)

            @block.scalar
            def _(eng):
                run("act", eng)

            @block.vector
            def _(eng):
                run("dve", eng)

            @block.gpsimd
            def _(eng):
                run("pool", eng)


class Builder:
    def __init__(self, nc, plan):
        self.nc = nc
        self.plan = plan
        self.P = Prog(nc)
        self.res_from_x = True
        self.light_ok = False

    def mm(self, out, lhsT, rhs, start, stop, reads, writes):
        self.P.op("pe", lambda e: e.matmul(out=out, lhsT=lhsT, rhs=rhs, start=start, stop=stop),
                  reads, writes)

    def tr(self, out, in_, ident, reads, writes):
        self.P.op("pe", lambda e: e.transpose(out=out, in_=in_, identity=ident), reads, writes)

    def act(self, out, in_, func, reads, writes, **kw):
        self.P.op("act", lambda e: e.activation(out=out, in_=in_, func=func, **kw), reads, writes)

    def tt(self, eng, out, in0, in1, op, reads, writes):
        self.P.op(eng, lambda e: e.tensor_tensor(out=out, in0=in0, in1=in1, op=op), reads, writes)

    def ts(self, eng, out, in0, s1, s2, op0, op1, reads, writes):
        if s2 is None:
            self.P.op(eng, lambda e: e.tensor_scalar(out=out, in0=in0, scalar1=s1, scalar2=None, op0=op0),
                      reads, writes)
        else:
            self.P.op(eng, lambda e: e.tensor_scalar(out=out, in0=in0, scalar1=s1, scalar2=s2, op0=op0, op1=op1),
                      reads, writes)

    def stt(self, out, in0, scalar, in1, op0, op1, reads, writes):
        self.P.op("dve", lambda e: e.scalar_tensor_tensor(out=out, in0=in0, scalar=scalar, in1=in1,
                                                           op0=op0, op1=op1), reads, writes)

    def cp(self, eng, out, in_, reads, writes):
        if eng == "act":
            self.act(out, in_, AF.Copy, reads, writes)
        else:
            self.P.op(eng, lambda e: e.tensor_copy(out=out, in_=in_), reads, writes)

    def memset(self, eng, ap, val, writes):
        self.P.op(eng, lambda e: e.memset(ap, val), (), writes)

    def carve(self, off, shape, dt):
        n = 1
        for s in shape[1:]:
            n *= s
        esz = 4 if dt in (F32, I32) else 2
        assert off % 4 == 0
        a = self.arena[:, off // 2: off // 2 + n * esz // 2]
        if dt != BF16:
            a = a.bitcast(dt)
        if len(shape) == 2:
            return a
        names = "abcdef"[: len(shape) - 1]
        kw = {names[i]: shape[1 + i] for i in range(len(shape) - 2)}
        return a.rearrange("p (%s) -> p %s" % (" ".join(names), " ".join(names)), **kw)

    def pbank(self, b, dt=F32):
        v = self.psum[:, b, :]
        if dt == BF16:
            v = v.bitcast(BF16)
        return v

    def build(self):
        nc = self.nc
        dr = lambda name, shape, dt, kind: nc.dram_tensor(name, shape, dt, kind=kind).ap()
        I = "ExternalInput"
        self.x = dr("x", [SEQ, D], F32, I)
        self.mem = dr("mem", [NMEM, D], F32, I)
        self.mem_norm_g = dr("mem_norm_g", [D], F32, I)
        self.mix_norm_g = dr("mix_norm_g", [DEPTH, D], F32, I)
        self.xattn_norm_g = dr("xattn_norm_g", [DEPTH, D], F32, I)
        self.mlp_norm_g = dr("mlp_norm_g", [DEPTH, D], F32, I)
        self.s5_a_re = dr("s5_a_re", [2, 64, 64], F32, I)
        self.s5_a_im = dr("s5_a_im", [2, 64, 64], F32, I)
        self.s5_log_dt = dr("s5_log_dt", [2, 64], F32, I)
        self.s5_b_re = dr("s5_b_re", [2, 64, 64, 16], F32, I)
        self.s5_b_im = dr("s5_b_im", [2, 64, 64, 16], F32, I)
        self.s5_c_re = dr("s5_c_re", [2, 64, 16, 64], F32, I)
        self.s5_c_im = dr("s5_c_im", [2, 64, 16, 64], F32, I)
        self.s5_d = dr("s5_d", [2, D], F32, I)
        self.s5_w_glu = dr("s5_w_glu", [2, D, 2 * D], F32, I)
        self.conv_w_in = dr("conv_w_in", [2, D, 3 * D], F32, I)
        self.conv_w = dr("conv_w", [2, 3, D], F32, I)
        self.conv_w_out = dr("conv_w_out", [2, D, D], F32, I)
        self.xa_w_q = dr("xa_w_q", [DEPTH, D, D], F32, I)
        self.xa_w_kv = dr("xa_w_kv", [DEPTH, D, 2 * D], F32, I)
        self.xa_w_o = dr("xa_w_o", [DEPTH, D, D], F32, I)
        self.mlp_w1 = dr("mlp_w1", [DEPTH, D, HID], F32, I)
        self.mlp_w2 = dr("mlp_w2", [DEPTH, HID, D], F32, I)
        self.final_norm_g = dr("final_norm_g", [D], F32, I)
        self.y = dr("y", [SEQ, D], F32, "ExternalOutput")
        self.gy = dr("gy", [SEQ, D], BF16, "Internal")
        self.hts = dr("hts", [NHB, 128, 8 * 512], BF16, "Internal")

        with contextlib.ExitStack() as st:
            ARENA_BYTES = 204 * 1024
            self.arena = st.enter_context(nc.sbuf_tensor("arena", [128, ARENA_BYTES // 2], BF16))
            self.psum = st.enter_context(nc.psum_tensor("psum", [128, 8, 512], F32))
            self._layout()
            self._consts()
            self._mem_prep()
            for item in self.plan:
                kind = item[0]
                if kind == "s5":
                    self.pass_s5(item[1])
                elif kind == "glu":
                    self.pass_glu(item[1])
                elif kind == "conv":
                    self.pass_conv(item[1])
                elif kind == "xattn":
                    self.pass_xattn(item[1])
                elif kind == "mlp":
                    self.pass_mlp(item[1], item[2])
                elif kind == "final":
                    self.pass_final()
                elif kind == "copy":
                    self.pass_copy()
            self.P.emit()

    def _layout(self):
        c = self.carve
        o = 0
        self.ident_bf = c(o, [128, 128], BF16); o += 256
        self.ident_f = c(o, [128, 128], F32); o += 512
        self.ones_bf = c(o, [128, 128], BF16); o += 256
        self.ioff = c(o, [128, 128], F32); o += 512
        self.ones_f = c(o, [128, 128], F32); o += 512
        self.mask_f = c(o, [128, 128], F32); o += 512
        self.gain = c(o, [128, D], F32); o += 4096
        self.memT = c(o, [128, 8, NMEM], BF16); o += 4096
        self.KT = c(o, [128, 8, NMEM], BF16); o += 4096
        self.V = c(o, [128, 2, D], BF16); o += 4096
        self.junk = c(o, [128, D], BF16); o += 2048
        self.ss = c(o, [128, 2, 8], F32); o += 64
        self.rstd = c(o, [128, 2, 8], F32); o += 64
        self.cw = c(o, [128, 8, 3], F32); o += 96
        self.bar = {"act": c(o, [128, 2], F32), "dve": c(o + 8, [128, 2], F32), "pool": c(o + 16, [128, 2], F32)}
        self.bar["src"] = self.ident_f[:, 0:2]
        o += 32
        o = (o + 63) // 64 * 64
        self.o_xb = o; o += 32768
        self.o_r2 = o; o += 82 * 1024
        self.o_ar = o; o += 65536
        assert o <= 204 * 1024, o
        self.XB = [c(self.o_xb + i * 16384, [128, 4, D], F32) for i in range(2)]
        self.XB32 = c(self.o_xb, [128, 8, D], F32)
        r2 = self.o_r2
        self.HB = [c(r2 + i * 8192, [128, 4, D], BF16) for i in range(2)]
        self.hT = [c(r2 + 16384 + i * 8192, [128, 8, 512], BF16) for i in range(2)]
        self.o_r3 = r2 + 32768
        self.STG = [c(self.o_r3 + 34816 + i * 8192, [128, 2048], F32) for i in range(2)]
        self.stg_i = 0
        self.stg_n = 0
        self.stg_call = 0
        self.A = [self.o_ar + i * 16384 for i in range(4)]

    def _consts(self):
        P = self.P
        self.memset("pool", self.ones_f, 1.0, ["ones_f"])
        P.op("pool", lambda e: e.affine_select(out=self.ident_f, in_=self.ones_f, pattern=[[1, 128]],
                                               compare_op=ALU.is_equal, fill=0.0, base=0, channel_multiplier=-1),
             ["ones_f"], ["ident_f"])
        self.cp("pool", self.ident_bf, self.ident_f, ["ident_f"], ["ident_bf"])
        self.cp("pool", self.ones_bf, self.ones_f, ["ones_f"], ["ones_bf"])
        self.memset("pool", self.ioff, 0.0, ["ioff"])
        self.cp("pool", self.ioff[0:64, 64:128], self.ident_f[0:64, 0:64], ["ident_f", "ioff"], ["ioff"])
        self.cp("pool", self.ioff[64:128, 0:64], self.ident_f[64:128, 64:128], ["ident_f", "ioff"], ["ioff"])
        P.op("pool", lambda e: e.affine_select(out=self.mask_f.rearrange("p (a b) -> p a b", a=8),
                                               in_=self.ones_f.rearrange("p (a b) -> p a b", a=8),
                                               pattern=[[16, 8], [0, 16]], compare_op=ALU.is_ge, fill=0.0,
                                               base=15, channel_multiplier=-1),
             ["ones_f"], ["mask_f"])

    def load_gain(self, vec_ap):
        self.P.dma("sp", self.gain, vec_ap.partition_broadcast(128), (), ["gain"])

    def load_w(self, dram_ap, off, K, N, keys):
        kt = K // 128
        view = self.carve(off, [128, kt, N], BF16)
        nch = (N + 2047) // 2048
        cw_ = N // nch
        wkeys = []
        self.stg_call += 1
        for k in range(kt):
            for ci in range(nch):
                src = dram_ap[k * 128:(k + 1) * 128, ci * cw_:(ci + 1) * cw_]
                dst = view[:, k, ci * cw_:(ci + 1) * cw_]
                key = ("Wc", self.stg_call, off, k, ci)
                wkeys.append(key)
                self.stg_n += 1
                if self.stg_n % 2 == 0:
                    self.P.dma("pool", dst, src, (), [key], semkey="W:" + keys[0])
                    continue
                i = self.stg_i % 2
                self.stg_i += 1
                st = self.STG[i][:, 0:cw_]
                self.P.dma("sp", st, src, (), [("stg", i)])
                self.cp("dve" if i == 0 else "act", dst, st, [("stg", i)], [key])
        return view, wkeys

    def src_std(self):
        t = self.x if self.res_from_x else self.y
        return t.rearrange("(hb p s) d -> hb p s d", p=128, s=4)

    def dst_std(self):
        return self.y.rearrange("(hb p s) d -> hb p s d", p=128, s=4)

    def pre_x0(self):
        self.load_xb(0, 0)
        self._skip0 = True

    def load_xb(self, hb, buf):
        if hb == 0 and getattr(self, "_skip0", False):
            self._skip0 = False
            return
        self.P.dma("sp", self.XB[buf], self.src_std()[hb], [("y", hb)], [("XB", buf)])

    def store_xb(self, hb, buf, final=False):
        self.P.dma("sp", self.dst_std()[hb], self.XB[buf], [("XB", buf)], [("y", hb)],
                   semkey=("st", buf), final=final)

    def norm(self, xb, ns, sbuf_i, out_fn, xkeys, okeys, junk_fn):
        ss = self.ss[:, sbuf_i, 0:ns]
        rs = self.rstd[:, sbuf_i, 0:ns]
        kss = [("ss", sbuf_i, s) for s in range(ns)]
        krs = ("rstd", sbuf_i)
        for s in range(ns):
            self.act(junk_fn(s), xb[:, s, :], AF.Square, xkeys, [kss[s]], accum_out=self.ss[:, sbuf_i, s:s + 1])
        self.ts("dve", rs, ss, 1.0 / D, EPS, ALU.mult, ALU.add, kss, [krs])
        self.act(rs, rs, AF.Sqrt, [krs], [krs])
        self.P.op("dve", lambda e: e.reciprocal(out=rs, in_=rs), [krs], [krs])
        for s in range(ns):
            o_ap, i0, i1 = out_fn(s)
            self.stt(o_ap, i0, self.rstd[:, sbuf_i, s:s + 1], i1, ALU.mult, ALU.mult,
                     list(xkeys) + [krs, "gain"], okeys)

    def std_norm(self, buf):
        xb = self.XB[buf]
        hbuf = self.HB[buf]
        self.norm(xb, 4, buf, lambda s: (hbuf[:, s, :], xb[:, s, :], self.gain), [("XB", buf)], [("HB", buf)],
                  lambda s: hbuf[:, s, :])

    def transposes(self, src, ns, dstT, skeys, dkeys, tbanks=(0,), tok_off=0):
        for j in range(8):
            q = tbanks[j % len(tbanks)]
            pt = self.pbank(q, BF16)[:, 0: ns * 128]
            for s in range(ns):
                self.tr(pt[:, s * 128:(s + 1) * 128], src[:, s, j * 128:(j + 1) * 128], self.ident_bf,
                        list(skeys) + ["ident_bf"], [("ps", q)])
            eng = "dve" if j % 2 == 0 else "act"
            self.cp(eng, dstT[:, j, tok_off: tok_off + ns * 128], pt, [("ps", q)], dkeys)

    def residual_out(self, buf, lhs_fn, nk, w_view, lkeys, wkeys, banks):
        xb = self.XB[buf]
        i = 0
        for s in range(4):
            for oh in range(2):
                b = banks[i % len(banks)]
                i += 1
                pb = self.pbank(b)
                for k in range(nk):
                    self.mm(pb, lhs_fn(k, s), w_view[:, k, oh * 512:(oh + 1) * 512], k == 0, k == nk - 1,
                            list(lkeys) + list(wkeys), [("ps", b)])
                xs = xb[:, s, oh * 512:(oh + 1) * 512]
                self.tt("dve", xs, pb, xs, ALU.add, [("ps", b), ("XB", buf)], [("XB", buf)])

    def _mem_prep(self):
        P = self.P
        self.load_gain(self.mem_norm_g)
        xb = self.XB[0]
        P.dma("sp", xb[:, 0:2, :], self.mem.rearrange("(kt p) d -> p kt d", p=128), (), [("XB", 0)])
        hbuf = self.HB[0]
        self.norm(xb, 2, 0, lambda s: (hbuf[:, s, :], xb[:, s, :], self.gain), [("XB", 0)], [("HB", 0)],
                  lambda s: hbuf[:, s, :])
        self.transposes(hbuf, 2, self.memT, [("HB", 0)], ["memT"])
        P.barrier(self.bar)

    def pass_mlp(self, li, hh):
        P = self.P
        P.barrier(self.bar, light=self.light_ok)
        self.load_gain(self.mlp_norm_g[li])
        self.pre_x0()
        w1, w1k = self.load_w(self.mlp_w1[li][:, hh * 2048:(hh + 1) * 2048], self.A[0], D, 2048, ["A0", "A1"])
        w2v = [None, None]
        aT = self.carve(self.o_r3, [128, 16, 512], BF16)
        rl = [self.carve(self.o_r3 + 16384 + i * 2048, [128, 512], F32) for i in range(2)]

        def stageA(hb):
            buf = hb % 2
            self.load_xb(hb, buf)
            hflat = self.hT[buf].rearrange("p a b -> p (a b)")
            if hh == 0:
                self.std_norm(buf)
                self.transposes(self.HB[buf], 4, self.hT[buf], [("HB", buf)], [("hT", buf)], tbanks=(0, 1))
                P.dma("sp", self.hts[hb], hflat, [("hT", buf)], [("hts", hb)], semkey=("hst", buf))
            else:
                P.dma("sp", hflat, self.hts[hb], [("hts", hb)], [("hT", buf)])

        def stageB(hb):
            buf = hb % 2
            for f in range(16):
                b = 2 + (f % 2)
                pb = self.pbank(b)
                for k in range(8):
                    self.mm(pb, w1[:, k, f * 128:(f + 1) * 128], self.hT[buf][:, k, :], k == 0, k == 7,
                            w1k + [("hT", buf)], [("ps", b)])
                r = rl[f % 2]
                self.act(r, pb, AF.Relu, [("ps", b)], [("rl", f % 2)])
                self.tt("dve", aT[:, f, :], r, r, ALU.mult, [("rl", f % 2)], [("aT", f)])

        def stageC(hb):
            buf = hb % 2
            self.residual_out(buf, lambda k, s: aT[:, k, s * 128:(s + 1) * 128], 16, w2v[0],
                              [("aT", f) for f in range(16)], w2v[1], [4, 5, 6, 7])
            self.store_xb(hb, buf)

        stageA(0)
        w2v[0], w2v[1] = self.load_w(self.mlp_w2[li][hh * 2048:(hh + 1) * 2048, :], self.A[2], 2048, D, ["A2", "A3"])
        for hb in range(NHB):
            stageB(hb)
            if hb + 1 < NHB:
                stageA(hb + 1)
            stageC(hb)
        self.res_from_x = False
        self.light_ok = True

    def pass_xattn(self, li):
        P = self.P
        P.barrier(self.bar, light=self.light_ok)
        self.load_gain(self.xattn_norm_g[li])
        self.pre_x0()
        wkv, wkvk = self.load_w(self.xa_w_kv[li], self.A[2], D, 2 * D, ["A2", "A3"])
        wq, wqk = self.load_w(self.xa_w_q[li], self.A[0], D, D, ["A0"])
        wov = [None, None]
        for m in range(8):
            b = 1 + (m % 2)
            pb = self.pbank(b)[:, 0:NMEM]
            for k in range(8):
                self.mm(pb, wkv[:, k, m * 128:(m + 1) * 128], self.memT[:, k, :], k == 0, k == 7,
                        wkvk + ["memT"], [("ps", b)])
            self.cp("act" if m % 2 else "dve", self.KT[:, m, :], pb, [("ps", b)], ["KT"])
        i = 0
        for kt in range(2):
            for oh in range(2):
                b = 3 + (i % 2)
                i += 1
                pb = self.pbank(b)
                for k in range(8):
                    self.mm(pb, self.memT[:, k, kt * 128:(kt + 1) * 128],
                            wkv[:, k, D + oh * 512: D + (oh + 1) * 512], k == 0, k == 7,
                            wkvk + ["memT"], [("ps", b)])
                self.cp("act" if i % 2 else "dve", self.V[:, kt, oh * 512:(oh + 1) * 512], pb, [("ps", b)], ["V"])

        r3 = self.o_r3
        qT = self.carve(r3, [128, 8, 512], BF16)
        oT = self.carve(r3 + 8192, [128, 8, 512], BF16)
        PT = [self.carve(r3 + 16384 + i * 2048, [128, 2, 512], BF16) for i in range(2)]
        rc = [self.carve(r3 + 20480 + i * 2048, [128, 512], F32) for i in range(2)]

        def stageA(hb):
            buf = hb % 2
            self.load_xb(hb, buf)
            self.std_norm(buf)
            self.transposes(self.HB[buf], 4, self.hT[buf], [("HB", buf)], [("hT", buf)])

        def scores(hd):
            par = hd % 2
            for kt in range(2):
                b = 2 + par * 2 + kt
                pb = self.pbank(b)
                for dd in range(2):
                    self.mm(pb, self.KT[:, 2 * hd + dd, kt * 128:(kt + 1) * 128], qT[:, 2 * hd + dd, :],
                            dd == 0, dd == 1, ["KT", ("qT", 2 * hd + dd)], [("ps", b)])
                self.act(PT[par][:, kt, :], pb, AF.Exp, [("ps", b)], [("PT", par, kt)])

        def pv(hd):
            par = hd % 2
            pkeys = [("PT", par, 0), ("PT", par, 1)]
            pbs = self.pbank(6)
            for kt in range(2):
                self.mm(pbs, self.ones_bf, PT[par][:, kt, :], kt == 0, kt == 1, pkeys + ["ones_bf"], [("ps", 6)])
            self.P.op("dve", lambda e: e.reciprocal(out=rc[par], in_=pbs), [("ps", 6)], [("rc", par)])
            for dv in range(2):
                b = 1 if dv == 0 else 7
                pb = self.pbank(b)
                for kt in range(2):
                    self.mm(pb, self.V[:, kt, (2 * hd + dv) * 128:(2 * hd + dv + 1) * 128], PT[par][:, kt, :],
                            kt == 0, kt == 1, pkeys + ["V"], [("ps", b)])
                self.tt("dve", oT[:, 2 * hd + dv, :], pb, rc[par], ALU.mult, [("ps", b), ("rc", par)],
                        [("oT", 2 * hd + dv)])

        def stageB(hb):
            buf = hb % 2
            for m in range(8):
                b = 6 + (m % 2)
                pb = self.pbank(b)
                for k in range(8):
                    self.mm(pb, wq[:, k, m * 128:(m + 1) * 128], self.hT[buf][:, k, :], k == 0, k == 7,
                            wqk + [("hT", buf)], [("ps", b)])
                self.act(qT[:, m, :], pb, AF.Copy, [("ps", b)], [("qT", m)], scale=1.0 / 16.0)
            scores(0)
            scores(1)
            pv(0)
            scores(2)
            pv(1)
            scores(3)
            pv(2)
            pv(3)

        def stageC(hb):
            buf = hb % 2
            self.residual_out(buf, lambda k, s: oT[:, k, s * 128:(s + 1) * 128], 8, wov[0],
                              [("oT", m) for m in range(8)], wov[1], [2, 3, 4, 5])
            self.store_xb(hb, buf)

        stageA(0)
        wov[0], wov[1] = self.load_w(self.xa_w_o[li], self.A[1], D, D, ["A1"])
        for hb in range(NHB):
            stageB(hb)
            if hb + 1 < NHB:
                stageA(hb + 1)
            stageC(hb)
        self.res_from_x = False
        self.light_ok = True

    def pass_conv(self, li):
        P = self.P
        j = li // 2
        P.barrier(self.bar, light=self.light_ok)
        self.load_gain(self.mix_norm_g[li])
        self.pre_x0()
        w_in, w_ink = self.load_w(self.conv_w_in[j], self.A[0], D, 3 * D, ["A0", "A1", "A2"])
        woutv = [None, None]
        with self.nc.allow_non_contiguous_dma(reason="tiny conv taps"):
            for k in range(3):
                P.dma("sp", self.cw[:, :, k], self.conv_w[j][k].rearrange("(m p) -> p m", p=128), (), ["cw"], allow_slow_non_contiguous=True)
        r3 = self.o_r3
        Z = self.carve(r3, [128, 8, 4, 132], F32)
        cs = [self.carve(r3 + 16896 + i * 2048, [128, 4, 128], F32) for i in range(2)]
        zc = [self.carve(r3 + 20992 + i * 2048, [128, 4, 128], F32) for i in range(2)]
        gT = self.carve(r3 + 25088, [128, 8, 512], BF16)
        self.memset("pool", Z, 0.0, [("Z", m) for m in range(8)])

        def stageA(hb):
            buf = hb % 2
            self.load_xb(hb, buf)
            self.std_norm(buf)
            self.transposes(self.HB[buf], 4, self.hT[buf], [("HB", buf)], [("hT", buf)])

        def stageB(hb):
            buf = hb % 2
            for m in range(8):
                par = m % 2
                bb, bc, bv = 1 + par * 3, 2 + par * 3, 3 + par * 3
                for X, b in ((1, bc), (2, bv), (0, bb)):
                    pb = self.pbank(b)
                    for k in range(8):
                        self.mm(pb, w_in[:, k, X * D + m * 128: X * D + (m + 1) * 128], self.hT[buf][:, k, :],
                                k == 0, k == 7, w_ink + [("hT", buf)], [("ps", b)])
                c3 = cs[par]
                z3 = zc[par]
                kz = ("Z", m)
                self.act(c3.rearrange("p a b -> p (a b)"), self.pbank(bc), AF.Copy, [("ps", bc)], [("cs", par)])
                Zm = Z[:, m, :, :]
                self.tt("dve", Zm[:, :, 1:129], self.pbank(bv).rearrange("p (a b) -> p a b", a=4), c3, ALU.mult,
                        [("ps", bv), ("cs", par)], [kz])
                w0 = self.cw[:, m, 0:1]
                w1 = self.cw[:, m, 1:2]
                w2 = self.cw[:, m, 2:3]
                kzc = ("zc", par)
                self.ts("pool", z3, Zm[:, :, 1:129], w2, None, ALU.mult, None, [kz, "cw"], [kzc])
                self.stt(z3[:, 1:4, :], Zm[:, 0:3, 1:129], w1, z3[:, 1:4, :], ALU.mult, ALU.add, [kz, kzc, "cw"], [kzc])
                self.stt(z3[:, 0, :], Zm[:, 3, 0:128], w1, z3[:, 0, :], ALU.mult, ALU.add, [kz, kzc, "cw"], [kzc])
                self.stt(z3[:, 2:4, :], Zm[:, 0:2, 1:129], w0, z3[:, 2:4, :], ALU.mult, ALU.add, [kz, kzc, "cw"], [kzc])
                self.stt(z3[:, 1, :], Zm[:, 3, 0:128], w0, z3[:, 1, :], ALU.mult, ALU.add, [kz, kzc, "cw"], [kzc])
                self.stt(z3[:, 0, :], Zm[:, 2, 0:128], w0, z3[:, 0, :], ALU.mult, ALU.add, [kz, kzc, "cw"], [kzc])
                self.tt("dve", gT[:, m, :], self.pbank(bb), z3.rearrange("p a b -> p (a b)"), ALU.mult,
                        [("ps", bb), kzc], [("gT", m)])
                self.cp("pool", Zm[:, 2:4, 0:1], Zm[:, 2:4, 128:129], [kz], [kz])

        def stageC(hb):
            buf = hb % 2
            self.residual_out(buf, lambda k, s: gT[:, k, s * 128:(s + 1) * 128], 8, woutv[0],
                              [("gT", m) for m in range(8)], woutv[1], [7, 1, 2, 3])
            self.store_xb(hb, buf)

        stageA(0)
        woutv[0], woutv[1] = self.load_w(self.conv_w_out[li // 2], self.A[3], D, D, ["A3"])
        for hb in range(NHB):
            stageB(hb)
            if hb + 1 < NHB:
                stageA(hb + 1)
            stageC(hb)
        self.res_from_x = False
        self.light_ok = True

    def pass_glu(self, li):
        P = self.P
        j = li // 2
        P.barrier(self.bar)
        self.pre_x0()
        wg, wgk = self.load_w(self.s5_w_glu[j], self.A[0], D, 2 * D, ["A0", "A1"])
        gyv = self.gy.rearrange("(hb p s) d -> hb p s d", p=128, s=4)
        r3 = self.o_r3
        sg = [self.carve(r3 + i * 2048, [128, 512], F32) for i in range(2)]
        tm = [self.carve(r3 + 4096 + i * 2048, [128, 512], F32) for i in range(2)]

        def stageA(hb):
            buf = hb % 2
            self.load_xb(hb, buf)
            P.dma("sp", self.HB[buf], gyv[hb], [("gy", hb)], [("HB", buf)])
            self.transposes(self.HB[buf], 4, self.hT[buf], [("HB", buf)], [("hT", buf)], tbanks=(0, 7))

        def stageC(hb):
            buf = hb % 2
            xb = self.XB[buf]
            for s in range(4):
                for oh in range(2):
                    par = oh
                    bv, bg = [(1, 2), (3, 4), (5, 6)][(s * 2 + oh) % 3]
                    for b, col in ((bv, oh * 512), (bg, D + oh * 512)):
                        pb = self.pbank(b)
                        for k in range(8):
                            self.mm(pb, self.hT[buf][:, k, s * 128:(s + 1) * 128], wg[:, k, col: col + 512],
                                    k == 0, k == 7, wgk + [("hT", buf)], [("ps", b)])
                    self.act(sg[par], self.pbank(bg), AF.Sigmoid, [("ps", bg)], [("sg", par)])
                    self.tt("dve", tm[par], self.pbank(bv), sg[par], ALU.mult, [("ps", bv), ("sg", par)], [("tm", par)])
                    xs = xb[:, s, oh * 512:(oh + 1) * 512]
                    self.tt("pool", xs, xs, tm[par], ALU.add, [("tm", par), ("XB", buf)], [("XB", buf)])
            self.store_xb(hb, buf)

        stageA(0)
        for hb in range(NHB):
            if hb + 1 < NHB:
                stageA(hb + 1)
            stageC(hb)
        self.res_from_x = False
        self.light_ok = True

    def pass_final(self):
        P = self.P
        P.barrier(self.bar, light=self.light_ok)
        self.load_gain(self.final_norm_g)
        for hb in range(NHB):
            buf = hb % 2
            self.load_xb(hb, buf)
            xb = self.XB[buf]
            hbuf = self.HB[buf]
            self.norm(xb, 4, buf, lambda s: (xb[:, s, :], xb[:, s, :], self.gain), [("XB", buf)], [("XB", buf)],
                      lambda s, hbuf=hbuf: hbuf[:, s, :])
            self.store_xb(hb, buf, final=True)
        self.res_from_x = False

    def pass_copy(self):
        self.P.barrier(self.bar)
        for hb in range(NHB):
            buf = hb % 2
            self.load_xb(hb, buf)
            self.store_xb(hb, buf, final=True)
        self.res_from_x = False

    def pass_s5(self, li):
        P = self.P
        nc = self.nc
        j = li // 2
        P.barrier(self.bar)
        self.light_ok = False
        self.load_gain(self.mix_norm_g[li])
        c = self.carve
        NK = 34
        r2 = self.o_r2
        T0T = c(r2, [128, 64, 128], BF16)
        BtT = c(r2 + 16384, [128, 64, 128], BF16)
        CtT = c(r2 + 32768, [128, 64, 128], BF16)
        GYt = [c(r2 + 49152 + i * 8192, [128, 4, 8, 128], BF16) for i in range(2)]
        U = [c(r2 + 65536 + i * 1024, [128, 512], BF16) for i in range(4)]
        XS = [c(r2 + 69632 + i * 1032, [128, 516], BF16) for i in range(4)]
        MR = [c(r2 + 73760 + i * 256, [128, 128], BF16) for i in range(18)]
        v1 = c(r2 + 78368, [128, 64, 9], F32)
        v2 = c(r2 + 80672, [128, 64, 9], F32)
        assert 80672 + 2304 <= 82 * 1024
        a = self.o_ar
        are = c(a, [128, 64], F32)
        aim = c(a + 256, [128, 64], F32)
        dtb = c(a + 512, [128, 64], F32)
        ktab = c(a + 768, [128, NK], F32)
        dcol = c(a + 1024, [128, 64], F32)
        ldr = c(a + 1280, [128, 64], F32)
        ldi = c(a + 1536, [128, 64], F32)
        fr = c(a + 1792, [128, 64], F32)
        fi = c(a + 2048, [128, 64], F32)
        t1 = c(a + 2304, [128, 64], F32)
        t2 = c(a + 2560, [128, 64], F32)
        o0 = a + 3072
        TB = 8704
        MAG = c(o0, [128, 64, NK], F32)
        U1 = c(o0 + TB, [128, 64, NK], F32)
        U2 = c(o0 + 2 * TB, [128, 64, NK], F32)
        o1 = o0 + 3 * TB
        Cre = c(o1, [128, 64, 16], F32)
        Cim = c(o1 + 4096, [128, 64, 16], F32)
        Bre = c(o1 + 8192, [128, 64, 16], F32)
        Bim = c(o1 + 12288, [128, 64, 16], F32)
        F1 = c(o1 + 16384, [128, 64, 9], F32)
        F2 = c(o1 + 16384 + 2304, [128, 64, 9], F32)
        G1 = c(o1 + 16384 + 4608, [128, 64, 16], F32)
        G2 = c(o1 + 16384 + 4608 + 4096, [128, 64, 16], F32)
        assert o1 + 16384 + 4608 + 8192 <= a + 65536
        xo = self.o_xb
        TI = c(xo, [128, 64 * NK], I32)
        TF = c(xo + TB, [128, 64 * NK], F32)
        AL = c(xo + 2 * TB, [128, 2, 128], F32)[0:64]
        CL = c(xo + 2 * TB + 1024, [128, 2, 8, 128], F32)
        ERB = c(xo + 2 * TB + 1024 + 8192, [128, 64, 16], F32)
        EIB = c(xo + 2 * TB + 1024 + 12288, [128, 64, 16], F32)
        assert 2 * TB + 1024 + 16384 <= 32768 + 4096
        WC = c(xo, [128, 8, 9, 16], F32)
        WCt = c(xo + 4608, [128, 8, 9, 16], F32)
        WB = c(xo + 9216, [128, 8, 16, 16], F32)
        WBt = c(xo + 17408, [128, 8, 16, 16], F32)
        T0s = [c(xo + 25600 + i * 512, [128, 128], F32) for i in range(2)]
        MT1 = [c(xo + 26624 + i * 512, [128, 128], F32) for i in range(2)]
        MT2 = [c(xo + 27648 + i * 512, [128, 128], F32) for i in range(2)]

        klist = list(range(0, 9)) + [-s for s in range(8)] + [7 - s for s in range(8)] + [8 * 2 ** l for l in range(9)]
        assert len(klist) == NK
        for i, kv in enumerate(klist):
            self.memset("pool", ktab[:, i:i + 1], float(kv), ["ktab"])
        for h in range(2):
            P.dma("sp", AL[:, 0, h * 64:(h + 1) * 64], self.s5_a_re[j], (), ["AL"])
            P.dma("sp", AL[:, 1, h * 64:(h + 1) * 64], self.s5_a_im[j], (), ["AL"])
        P.dma("sp", dtb, self.s5_log_dt[j].partition_broadcast(128), (), ["dtb"])
        for t, dst in ((0, are), (1, aim)):
            pb = self.pbank(1 + t)[:, 0:64]
            self.tr(pb, AL[:, t, :], self.ident_f[0:64, 0:64], ["AL", "ident_f"], [("ps", 1 + t)])
            self.cp("dve", dst, pb, [("ps", 1 + t)], ["a%d" % t])
        self.act(dtb, dtb, AF.Exp, ["dtb"], ["dtb"])
        self.tt("dve", ldr, are, dtb, ALU.mult, ["a0", "dtb"], ["ld0"])
        self.tt("dve", ldi, aim, dtb, ALU.mult, ["a1", "dtb"], ["ld1"])
        kb = ktab.unsqueeze(1).to_broadcast([128, 64, NK])
        self.tt("dve", MAG, ldr.unsqueeze(2).to_broadcast([128, 64, NK]), kb, ALU.mult, ["ld0", "ktab"], ["MAG"])
        self.tt("dve", U1, ldi.unsqueeze(2).to_broadcast([128, 64, NK]), kb, ALU.mult, ["ld1", "ktab"], ["U1"])
        self.act(MAG, MAG, AF.Exp, ["MAG"], ["MAG"])
        U1f = U1.rearrange("p a b -> p (a b)")
        U2f = U2.rearrange("p a b -> p (a b)")
        MAGf = MAG.rearrange("p a b -> p (a b)")
        self.ts("dve", U1f, U1f, 1.0 / (2 * math.pi), None, ALU.mult, None, ["U1"], ["U1"])
        self.ts("dve", U2f, U1f, 0.25, None, ALU.add, None, ["U1"], ["U2"])
        for Ux, kx in ((U1f, "U1"), (U2f, "U2")):
            self.cp("dve", TI, Ux, [kx], ["TI"])
            self.cp("dve", TF, TI, ["TI"], ["TF"])
            self.tt("dve", Ux, Ux, TF, ALU.subtract, [kx, "TF"], [kx])
            self.ts("dve", TF, Ux, 0.5, None, ALU.is_ge, None, [kx], ["TF"])
            self.tt("dve", Ux, Ux, TF, ALU.subtract, [kx, "TF"], [kx])
            self.ts("dve", TF, Ux, -0.5, None, ALU.is_lt, None, [kx], ["TF"])
            self.tt("dve", Ux, Ux, TF, ALU.add, [kx, "TF"], [kx])
            self.ts("dve", Ux, Ux, 2 * math.pi, None, ALU.mult, None, [kx], [kx])
            self.act(Ux, Ux, AF.Sin, [kx], [kx])
        self.tt("dve", U1f, U1f, MAGf, ALU.mult, ["U1", "MAG"], ["U1"])
        self.tt("dve", U2f, U2f, MAGf, ALU.mult, ["U2", "MAG"], ["U2"])
        EIM, ERE = U1, U2
        NEI = MAG
        self.ts("dve", MAGf, U1f, -1.0, None, ALU.mult, None, ["U1", "MAG"], ["MAG"])
        nr, ni = t1, t2
        self.ts("dve", nr, ERE[:, :, 1], -1.0, None, ALU.add, None, ["U2"], ["nr"])
        self.cp("dve", ni, EIM[:, :, 1], ["U1"], ["ni"])
        den = fi
        self.tt("dve", den, are, are, ALU.mult, ["a0"], ["den"])
        self.tt("dve", fr, aim, aim, ALU.mult, ["a1"], ["fr"])
        self.tt("dve", den, den, fr, ALU.add, ["den", "fr"], ["den"])
        self.P.op("dve", lambda e: e.reciprocal(out=dtb, in_=den), ["den", "dtb"], ["rden"])
        rden = dtb
        self.tt("dve", fr, nr, are, ALU.mult, ["nr", "a0", "fr"], ["fr"])
        self.tt("dve", ldr, ni, aim, ALU.mult, ["ni", "a1", "ld0", "MAG"], ["ld0"])
        self.tt("dve", fr, fr, ldr, ALU.add, ["fr", "ld0"], ["fr"])
        self.tt("dve", fr, fr, rden, ALU.mult, ["fr", "rden"], ["fr"])
        self.tt("dve", fi, ni, are, ALU.mult, ["ni", "a0", "den", "rden"], ["fi"])
        self.tt("dve", ldr, nr, aim, ALU.mult, ["nr", "a1", "ld0", "fr"], ["ld0"])
        self.tt("dve", fi, fi, ldr, ALU.subtract, ["fi", "ld0"], ["fi"])
        self.tt("dve", fi, fi, rden, ALU.mult, ["fi", "rden"], ["fi"])
        b3 = [128, 64, 16]
        frb = fr.unsqueeze(2).to_broadcast(b3)
        fib = fi.unsqueeze(2).to_broadcast(b3)
        self.tt("dve", ERB, ERE[:, :, 9:25], frb, ALU.mult, ["U2", "fr"], ["ERB"])
        self.tt("dve", G1, EIM[:, :, 9:25], fib, ALU.mult, ["U1", "fi"], ["G1"])
        self.tt("dve", ERB, ERB, G1, ALU.subtract, ["ERB", "G1"], ["ERB"])
        self.tt("dve", EIB, ERE[:, :, 9:25], fib, ALU.mult, ["U2", "fi"], ["EIB"])
        self.tt("dve", G2, EIM[:, :, 9:25], frb, ALU.mult, ["U1", "fr", "G1"], ["G2"])
        self.tt("dve", EIB, EIB, G2, ALU.add, ["EIB", "G2"], ["EIB"])
        lo, hi = slice(0, 64), slice(64, 128)
        self.cp("pool", F1[lo], ERE[lo, :, 0:9], ["U2"], ["F1"])
        self.cp("pool", F1[hi], NEI[hi, :, 0:9], ["MAG"], ["F1"])
        self.cp("pool", F2[lo], NEI[lo, :, 0:9], ["MAG"], ["F2"])
        self.ts("pool", F2[hi], ERE[hi, :, 0:9], -1.0, None, ALU.mult, None, ["U2"], ["F2"])
        self.cp("pool", G1[lo], ERB[lo], ["ERB", "G1", "EIB"], ["G1"])
        self.cp("pool", G1[hi], EIB[hi], ["EIB", "G1"], ["G1"])
        self.ts("pool", G2[lo], EIB[lo], -1.0, None, ALU.mult, None, ["EIB", "G2"], ["G2"])
        self.cp("pool", G2[hi], ERB[hi], ["ERB", "G2"], ["G2"])
        self.cp("pool", v1, ERE[:, :, 25:34], ["U2"], ["v1"])
        self.cp("pool", v2[lo], EIM[lo, :, 25:34], ["U1"], ["v2"])
        self.cp("pool", v2[hi], NEI[hi, :, 25:34], ["MAG"], ["v2"])
        for t, src in ((0, self.s5_c_re), (1, self.s5_c_im)):
            sv = src[j].rearrange("(gt gl) h p -> (gl h) gt p", gl=8)
            for h in range(2):
                P.dma("sp", CL[:, t, :, h * 64:(h + 1) * 64], sv, (), [("CL", t)])
        for t, dst in ((0, Cre), (1, Cim)):
            for gt in range(8):
                b = 3 + (gt % 2)
                pb = self.pbank(b)[:, 0:128]
                self.tr(pb, CL[:, t, gt, :], self.ident_f, [("CL", t), "ident_f"], [("ps", b)])
                self.cp("dve" if gt % 2 else "act", dst[:, gt * 8:(gt + 1) * 8, :].rearrange("p a b -> p (a b)"), pb,
                        [("ps", b)], ["C%d" % t])
        with nc.allow_non_contiguous_dma(reason="small s5 params"):
            for t, (src, dst) in enumerate(((self.s5_b_re, Bre), (self.s5_b_im, Bim))):
                sv = src[j].rearrange("g p h -> p g h")
                for h in range(2):
                    P.dma("sp", dst[h * 64:(h + 1) * 64], sv, (), ["B%d" % t], allow_slow_non_contiguous=True)
            dv = self.s5_d[j].rearrange("(g h) -> h g", h=16)
            for s in range(8):
                P.dma("sp", dcol[s * 16:(s + 1) * 16, :], dv, (), ["dcol"], allow_slow_non_contiguous=True)
        P.barrier(self.bar)
        for gb in range(8):
            g0 = gb * 8
            gs = slice(g0, g0 + 8)
            b4c = [128, 8, 9, 16]
            b4b = [128, 8, 16, 16]
            self.tt("dve", WC, Cre[:, gs, :].unsqueeze(2).to_broadcast(b4c), F1[:, gs, :].unsqueeze(3).to_broadcast(b4c),
                    ALU.mult, ["C0", "F1"], ["WC"])
            self.tt("dve", WCt, Cim[:, gs, :].unsqueeze(2).to_broadcast(b4c), F2[:, gs, :].unsqueeze(3).to_broadcast(b4c),
                    ALU.mult, ["C1", "F2"], ["WCt"])
            self.tt("dve", WC, WC, WCt, ALU.add, ["WC", "WCt"], ["WC"])
            self.tt("dve", WB, Bre[:, gs, :].unsqueeze(2).to_broadcast(b4b), G1[:, gs, :].unsqueeze(3).to_broadcast(b4b),
                    ALU.mult, ["B0", "G1"], ["WB"])
            self.tt("dve", WBt, Bim[:, gs, :].unsqueeze(2).to_broadcast(b4b), G2[:, gs, :].unsqueeze(3).to_broadcast(b4b),
                    ALU.mult, ["B1", "G2"], ["WBt"])
            self.tt("dve", WB, WB, WBt, ALU.add, ["WB", "WBt"], ["WB"])
            self.cp("act", CtT[:, gs, :].rearrange("p g (s h) -> p g s h", s=8), WC[:, :, 1:9, :], ["WC"], ["CtT"])
            for gl in range(8):
                g = g0 + gl
                par = gl % 2
                pb = self.pbank(1 + par)[:, 0:128]
                self.mm(pb, WB[:, gl, 0:8, :].rearrange("p a b -> p (a b)"),
                        WC[:, gl, 0:8, :].rearrange("p a b -> p (a b)"), True, True, ["WB", "WC"], [("ps", 1 + par)])
                self.tt("dve", T0s[par], pb, self.mask_f, ALU.mult, [("ps", 1 + par), "mask_f"], [("T0s", par)])
                self.stt(T0T[:, g, :], self.ident_f, dcol[:, g:g + 1], T0s[par], ALU.mult, ALU.add,
                         [("T0s", par), "dcol", "ident_f"], ["T0T"])
                pb2 = self.pbank(3 + par)[:, 0:128]
                self.tr(pb2, WB[:, gl, 8:16, :].rearrange("p a b -> p (a b)"), self.ident_f, ["WB", "ident_f"],
                        [("ps", 3 + par)])
                self.cp("act", BtT[:, g, :], pb2, [("ps", 3 + par)], ["BtT"])
        P.barrier(self.bar)

        Hc = [c(self.o_ar + r * 16384, [128, 64, 8, 16], BF16) for r in range(4)]
        src = (self.x if self.res_from_x else self.y).rearrange("(r p s) d -> r p s d", p=128, s=8)
        xb = self.XB32
        g3 = self.gain.rearrange("p (g h) -> p g h", h=16)
        for r in range(4):
            P.dma("sp", xb[:, 0:4, :], src[r][:, 0:4, :], [("y", 2 * r), ("y", 2 * r + 1)], ["XB32"])
            P.dma("sp", xb[:, 4:8, :], src[r][:, 4:8, :], [("y", 2 * r), ("y", 2 * r + 1)], ["XB32b"])
            self.norm(xb, 8, 0,
                      lambda s, r=r: (Hc[r][:, :, s, :], xb[:, s, :].rearrange("p (g h) -> p g h", h=16), g3),
                      ["XB32", "XB32b"], [("Hc", r)],
                      lambda s, r=r: Hc[r].rearrange("p g s h -> p (g s h)")[:, s * 1024:(s + 1) * 1024])
        for q in range(4):
            self.memset("pool", XS[q][:, 0:1], 0.0, [("XS", q)])
        P.barrier(self.bar)
        Mt1 = c(xo, [128, 9, 128], F32)
        Mt2 = c(xo + 4608, [128, 9, 128], F32)
        MRq = [c(xo + 9216 + i * 2304, [128, 9, 128], BF16) for i in range(4)]

        gyw = self.gy.rearrange("(r p s) d -> p r s d", p=128, s=8)
        for gb4 in range(16):
            groups = [gb4 * 4 + q for q in range(4)]
            for q, g in enumerate(groups):
                ub = U[q]
                sl = g % 2
                pt = self.pbank(7 * sl, BF16)[:, 0:512]
                for r in range(4):
                    self.tr(pt[:, r * 128:(r + 1) * 128], Hc[r][:, g, :, :].rearrange("p a b -> p (a b)"),
                            self.ident_bf, [("Hc", r), "ident_bf"], [("ps", 7 * sl)])
                self.cp("act", ub, pt, [("ps", 7 * sl)], [("U", q)])
                b = 1 + q
                pb = self.pbank(b)
                self.mm(pb, BtT[:, g, :], ub, True, True, ["BtT", ("U", q)], [("ps", b)])
                self.cp("act", XS[q][:, 1:513], pb, [("ps", b)], [("XS", q)])
            for lev in range(9):
                d = 1 << lev
                for q, g in enumerate(groups):
                    m1 = Mt1[:, q % 2, :]
                    m2 = Mt2[:, q % 2, :]
                    self.act(m1, self.ident_f, AF.Copy, ["ident_f", "v1"], [("MT1", q % 2)], scale=v1[:, g, lev:lev + 1])
                    self.act(m2, self.ioff, AF.Copy, ["ioff", "v2"], [("MT2", q % 2)], scale=v2[:, g, lev:lev + 1])
                    self.tt("pool", MRq[q][:, lev, :], m1, m2, ALU.add, [("MT1", q % 2), ("MT2", q % 2)], [("MRq", q, lev)])
                    b = 1 + q
                    pb = self.pbank(b)[:, 0:512 - d]
                    self.mm(pb, MRq[q][:, lev, :], XS[q][:, 1:513 - d], True, True, [("MRq", q, lev), ("XS", q)], [("ps", b)])
                    xs_hi = XS[q][:, 1 + d:513]
                    self.tt("dve", xs_hi, xs_hi, pb, ALU.add, [("ps", b), ("XS", q)], [("XS", q)])
            for q, g in enumerate(groups):
                gl = g % 8
                jt = g // 8
                jb = jt % 2
                yb = 5 + (g % 2)
                pby = self.pbank(yb)
                for r in range(4):
                    o_ap = pby[:, r * 128:(r + 1) * 128]
                    self.mm(o_ap, U[q][:, r * 128:(r + 1) * 128], T0T[:, g, :], True, False,
                            [("U", q), "T0T"], [("ps", yb)])
                    self.mm(o_ap, XS[q][:, r * 128:(r + 1) * 128], CtT[:, g, :], False, True,
                            [("XS", q), "CtT"], [("ps", yb)])
                for r in range(4):
                    self.act(GYt[jb][:, r, :, gl * 16:(gl + 1) * 16],
                             pby[:, r * 128:(r + 1) * 128].rearrange("p (s h) -> p s h", h=16),
                             AF.Gelu_apprx_tanh, [("ps", yb)], [("GYt", jb)])
                if gl == 7:
                    for r in range(4):
                        P.dma("sp", gyw[:, r, :, jt * 128:(jt + 1) * 128], GYt[jb][:, r, :, :], [("GYt", jb)],
                              [("gy", hh) for hh in range(8)], semkey=("gyst", jb))


FULL_PLAN = []
for _li in range(DEPTH):
    if _li % 2 == 0:
        FULL_PLAN += [("s5", _li), ("glu", _li)]
    else:
        FULL_PLAN += [("conv", _li)]
    FULL_PLAN += [("xattn", _li), ("mlp", _li, 0), ("mlp", _li, 1)]
FULL_PLAN += [("final",)]

WEIGHT_NAMES = ["mem_norm_g", "mix_norm_g", "xattn_norm_g", "mlp_norm_g", "s5_a_re", "s5_a_im", "s5_log_dt",
                "s5_b_re", "s5_b_im", "s5_c_re", "s5_c_im", "s5_d", "s5_w_glu", "conv_w_in", "conv_w",
                "conv_w_out", "xa_w_q", "xa_w_kv", "xa_w_o", "mlp_w1", "mlp_w2", "final_norm_g"]


def build_nc(plan):
    nc = bass.Bass("TRN2", target_bir_lowering=False)
    Builder(nc, plan).build()
    return nc


def run_plan(plan, inputs, n_cores=8, trace=False):
    nc = build_nc(plan)
    shared = {k: np.ascontiguousarray(np.asarray(inputs[k], dtype=np.float32)) for k in WEIGHT_NAMES}
    x = np.asarray(inputs["x"], dtype=np.float32)
    mem = np.asarray(inputs["mem"], dtype=np.float32)
    in_maps = []
    for b in range(n_cores):
        m = dict(shared)
        m["x"] = np.ascontiguousarray(x[b])
        m["mem"] = np.ascontiguousarray(mem[b])
        in_maps.append(m)
    res = run_bass_kernel_spmd(nc, in_maps, core_ids=list(range(n_cores)), trace=trace)
    out = np.stack([np.asarray(r["y"]) for r in res.results], axis=0)
    return out, res


def kernel(**inputs):
    out, _ = run_plan(FULL_PLAN, inputs, n_cores=8)
    return out.astype(np.float32)
```
